# Optimizing a Trainium2 kernel written in Bass

```python
import math
import jax, jax.numpy as jnp
from jax import lax
import numpy as np

D_MODEL = 1024
BATCH = 16
SEQ = 2048
DEPTH = 1
DEC_BATCH = 2
DEC_SEQ = 16384
PAST_LEN = 128

D_FF = 2816
PLE_DIM = 256
HA = 8
Q_LORA = 384
KV_LORA = 256
NOPE_DIM = 64
ROPE_DIM = 32
V_DIM = 64
ROPE_THETA = 10000.0
Q_BLOCK = 128
HB = 8
KVH = 2
REP = HB // KVH
HD = 64
WINDOW = 128
W_BLOCK = 128
REL_BUCKETS = 32
REL_MAX_DIST = 128
IN_SPLITS = (Q_LORA, KV_LORA, ROPE_DIM, HB * HD, KVH * HD, KVH * HD, D_MODEL, D_MODEL)
IN_COLS = sum(IN_SPLITS)
EPS = 1e-6
NEG = -1e30

kernel_name = "hybrid_mla_window_gqa_encoder"


def rmsnorm(x, g):
    xf = x.astype(jnp.float32)
    y = xf * lax.rsqrt(jnp.mean(xf * xf, axis=-1, keepdims=True) + EPS)
    return (y * g.astype(jnp.float32)).astype(x.dtype)


def swiglu(x, w1, w3, w2):
    return (jax.nn.silu(x @ w1) * (x @ w3)) @ w2


def rope_tables(S):
    inv = 1.0 / (ROPE_THETA ** (jnp.arange(0, ROPE_DIM, 2, dtype=jnp.float32) / ROPE_DIM))
    ang = jnp.arange(S, dtype=jnp.float32)[:, None] * inv[None, :]
    return jnp.cos(ang), jnp.sin(ang)


def apply_rope(x, cos, sin):
    x1, x2 = jnp.split(x, 2, axis=-1)
    shape = (1, cos.shape[0]) + (1,) * (x.ndim - 3) + (cos.shape[1],)
    c = cos.reshape(shape).astype(x.dtype)
    s = sin.reshape(shape).astype(x.dtype)
    return jnp.concatenate([x1 * c - x2 * s, x1 * s + x2 * c], axis=-1)


def t5_bucket(rel):
    nb = REL_BUCKETS // 2
    max_exact = nb // 2
    ret = jnp.where(rel > 0, nb, 0)
    n = jnp.abs(rel)
    nf = jnp.maximum(n, 1).astype(jnp.float32)
    large = max_exact + (jnp.log(nf / max_exact) / math.log(REL_MAX_DIST / max_exact)
                         * (nb - max_exact)).astype(jnp.int32)
    large = jnp.minimum(large, nb - 1)
    return ret + jnp.where(n < max_exact, n, large)


def mla_attention(c_q, c_kv, k_rope_raw, q_norm_g, kv_norm_g, w_uq, w_uk, w_uv):
    B, S, _ = c_q.shape
    cq = rmsnorm(c_q, q_norm_g)
    ckv = rmsnorm(c_kv, kv_norm_g)
    q = (cq @ w_uq).reshape(B, S, HA, NOPE_DIM + ROPE_DIM)
    cos, sin = rope_tables(S)
    q_nope = q[..., :NOPE_DIM]
    q_rope = apply_rope(q[..., NOPE_DIM:], cos, sin)
    k_rope = apply_rope(k_rope_raw, cos, sin)
    k_nope = (ckv @ w_uk).reshape(B, S, HA, NOPE_DIM)
    v = (ckv @ w_uv).reshape(B, S, HA, V_DIM)
    scale = (NOPE_DIM + ROPE_DIM) ** -0.5
    nq = S // Q_BLOCK
    qn = q_nope.reshape(B, nq, Q_BLOCK, HA, NOPE_DIM).transpose(1, 0, 2, 3, 4)
    qr = q_rope.reshape(B, nq, Q_BLOCK, HA, ROPE_DIM).transpose(1, 0, 2, 3, 4)

    def block(args):
        qn_b, qr_b = args
        s = (jnp.einsum('bqhd,bkhd->bhqk', qn_b, k_nope)
             + jnp.einsum('bqhr,bkr->bhqk', qr_b, k_rope))
        p = jax.nn.softmax(s.astype(jnp.float32) * scale, axis=-1).astype(v.dtype)
        return jnp.einsum('bhqk,bkhd->bqhd', p, v)

    o = lax.map(block, (qn, qr))
    return o.transpose(1, 0, 2, 3, 4).reshape(B, S, HA * V_DIM)


def window_attention(q, k, v, sink, rel_bias):
    B, S, _ = q.shape
    nb = S // W_BLOCK
    q = q.reshape(B, nb, W_BLOCK, KVH, REP, HD)
    pad = ((0, 0), (W_BLOCK, W_BLOCK), (0, 0), (0, 0))
    kp = jnp.pad(k.reshape(B, S, KVH, HD), pad)
    vp = jnp.pad(v.reshape(B, S, KVH, HD), pad)

    def bands(t):
        t = t.reshape(B, nb + 2, W_BLOCK, KVH, HD)
        return jnp.concatenate([t[:, :-2], t[:, 1:-1], t[:, 2:]], axis=2)

    kb = bands(kp)
    vb = bands(vp)
    s = jnp.einsum('bnqgrd,bnkgd->bngrqk', q, kb).astype(jnp.float32) * (HD ** -0.5)
    rel = (jnp.arange(3 * W_BLOCK, dtype=jnp.int32)[None, :] - W_BLOCK
           - jnp.arange(W_BLOCK, dtype=jnp.int32)[:, None])
    kpos = (jnp.arange(nb, dtype=jnp.int32)[:, None] * W_BLOCK - W_BLOCK
            + jnp.arange(3 * W_BLOCK, dtype=jnp.int32)[None, :])
    valid = (jnp.abs(rel) <= WINDOW)[None] & ((kpos >= 0) & (kpos < S))[:, None, :]
    bias = rel_bias[t5_bucket(rel)].astype(jnp.float32)
    bias = bias.transpose(2, 0, 1).reshape(KVH, REP, W_BLOCK, 3 * W_BLOCK)
    s = jnp.where(valid[None, :, None, None], s + bias[None, None], NEG)
    sink_col = jnp.broadcast_to(sink.astype(jnp.float32).reshape(KVH, REP, 1, 1),
                                s.shape[:-1] + (1,))
    p = jax.nn.softmax(jnp.concatenate([s, sink_col], axis=-1), axis=-1)[..., :-1]
    o = jnp.einsum('bngrqk,bnkgd->bnqgrd', p.astype(vb.dtype), vb)
    return o.reshape(B, S, HB * HD)


def encoder_layer(x, p, rel_bias,
                  ffn1_pre_g, ffn1_post_g, ffn1_w1, ffn1_w3, ffn1_w2,
                  mix_pre_g, mix_post_g, w_in, q_norm_g, kv_norm_g, w_uq, w_uk, w_uv,
                  sink, w_proj_a, w_proj_b, w_out,
                  ffn2_pre_g, ffn2_post_g, ffn2_w1, ffn2_w3, ffn2_w2,
                  ple_pre_g, ple_post_g, w_ple_gate, w_ple_proj):
    h = x + 0.5 * rmsnorm(swiglu(rmsnorm(x, ffn1_pre_g), ffn1_w1, ffn1_w3, ffn1_w2), ffn1_post_g)
    u = rmsnorm(h, mix_pre_g)
    z = u @ w_in
    idx = list(np.cumsum(IN_SPLITS)[:-1])
    c_q, c_kv, k_rope, q_b, k_b, v_b, g_a, g_b = jnp.split(z, idx, axis=-1)
    y_a = mla_attention(c_q, c_kv, k_rope, q_norm_g, kv_norm_g, w_uq, w_uk, w_uv)
    y_b = window_attention(q_b, k_b, v_b, sink, rel_bias)
    m = jax.nn.sigmoid(g_a) * (y_a @ w_proj_a) + jax.nn.sigmoid(g_b) * (y_b @ w_proj_b)
    h = h + rmsnorm(m @ w_out, mix_post_g)
    h = h + 0.5 * rmsnorm(swiglu(rmsnorm(h, ffn2_pre_g), ffn2_w1, ffn2_w3, ffn2_w2), ffn2_post_g)
    e = (p @ w_ple_proj) * jax.nn.sigmoid(rmsnorm(h, ple_pre_g) @ w_ple_gate)
    return h + rmsnorm(e, ple_post_g)


def setup_inputs(seed: int = 0) -> dict:
    key = jax.random.key(seed)
    ks = iter(jax.random.split(key, 64))
    f32 = jnp.float32

    def w(shape, fan_in):
        return jax.random.normal(next(ks), shape, f32) * (fan_in ** -0.5)

    def g(n):
        return 1.0 + 0.05 * jax.random.normal(next(ks), (DEPTH, n), f32)

    L = DEPTH
    return {
        "x_prompt": jax.random.normal(next(ks), (BATCH, SEQ, D_MODEL), f32),
        "x_sample": jax.random.normal(next(ks), (DEC_BATCH, DEC_SEQ, D_MODEL), f32),
        "p_prompt": jax.random.normal(next(ks), (DEPTH, BATCH, SEQ, PLE_DIM), f32),
        "p_sample": jax.random.normal(next(ks), (DEPTH, DEC_BATCH, DEC_SEQ, PLE_DIM), f32),
        "rel_bias": 0.5 * jax.random.normal(next(ks), (REL_BUCKETS, HB), f32),
        "ffn1_pre_g": g(D_MODEL),
        "ffn1_post_g": g(D_MODEL),
        "ffn1_w1": w((L, D_MODEL, D_FF), D_MODEL),
        "ffn1_w3": w((L, D_MODEL, D_FF), D_MODEL),
        "ffn1_w2": w((L, D_FF, D_MODEL), D_FF),
        "mix_pre_g": g(D_MODEL),
        "mix_post_g": g(D_MODEL),
        "w_in": w((L, D_MODEL, IN_COLS), D_MODEL),
        "q_norm_g": g(Q_LORA),
        "kv_norm_g": g(KV_LORA),
        "w_uq": w((L, Q_LORA, HA * (NOPE_DIM + ROPE_DIM)), Q_LORA),
        "w_uk": w((L, KV_LORA, HA * NOPE_DIM), KV_LORA),
        "w_uv": w((L, KV_LORA, HA * V_DIM), KV_LORA),
        "sink": 0.5 * jax.random.normal(next(ks), (L, HB), f32),
        "w_proj_a": w((L, HA * V_DIM, D_MODEL), HA * V_DIM),
        "w_proj_b": w((L, HB * HD, D_MODEL), HB * HD),
        "w_out": w((L, D_MODEL, D_MODEL), D_MODEL),
        "ffn2_pre_g": g(D_MODEL),
        "ffn2_post_g": g(D_MODEL),
        "ffn2_w1": w((L, D_MODEL, D_FF), D_MODEL),
        "ffn2_w3": w((L, D_MODEL, D_FF), D_MODEL),
        "ffn2_w2": w((L, D_FF, D_MODEL), D_FF),
        "ple_pre_g": g(D_MODEL),
        "ple_post_g": g(D_MODEL),
        "w_ple_gate": w((L, D_MODEL, D_MODEL), D_MODEL),
        "w_ple_proj": w((L, PLE_DIM, D_MODEL), PLE_DIM),
    }


def reference(x_prompt, x_sample, p_prompt, p_sample, rel_bias,
              ffn1_pre_g, ffn1_post_g, ffn1_w1, ffn1_w3, ffn1_w2,
              mix_pre_g, mix_post_g, w_in, q_norm_g, kv_norm_g, w_uq, w_uk, w_uv,
              sink, w_proj_a, w_proj_b, w_out,
              ffn2_pre_g, ffn2_post_g, ffn2_w1, ffn2_w3, ffn2_w2,
              ple_pre_g, ple_post_g, w_ple_gate, w_ple_proj):
    stacked = (ffn1_pre_g, ffn1_post_g, ffn1_w1, ffn1_w3, ffn1_w2,
               mix_pre_g, mix_post_g, w_in, q_norm_g, kv_norm_g, w_uq, w_uk, w_uv,
               sink, w_proj_a, w_proj_b, w_out,
               ffn2_pre_g, ffn2_post_g, ffn2_w1, ffn2_w3, ffn2_w2,
               ple_pre_g, ple_post_g, w_ple_gate, w_ple_proj)
    y_prompt = x_prompt
    y_sample = x_sample
    for i in range(DEPTH):
        lw = [t[i] for t in stacked]
        y_prompt = encoder_layer(y_prompt, p_prompt[i], rel_bias, *lw)
        y_sample = encoder_layer(y_sample, p_sample[i], rel_bias, *lw)
    return (y_prompt, y_sample)
```

```python
import types
import numpy as np
import ml_dtypes
from contextlib import ExitStack
import concourse.bass as bass
import concourse.mybir as mybir
from concourse.bass_utils import run_bass_kernel_spmd

F32 = mybir.dt.float32
BF16 = mybir.dt.bfloat16
AF = mybir.ActivationFunctionType
ALU = mybir.AluOpType

D = 1024
DFF = 2816
NF = 22
PLE = 256
EPS = 1e-6
NEG = -1e30
NRING = 48
ENG = ['sp', 'pe', 'act', 'dve', 'pool']


class Cfg:
    def __init__(s, NP=2, SP=2048, SS=16384, QS=4096):
        s.NP, s.SP, s.SS, s.QS = NP, SP, SS, QS
        s.jobs = []
        tb = 0
        qb = 0
        for i in range(NP):
            s.jobs.append(dict(name='p%d' % i, ntok=SP, q0=0, nq=SP, sample=False, tb=tb, qb=qb))
            tb += SP
            qb += SP
        s.jobs.append(dict(name='s', ntok=SS, q0=512, nq=QS, sample=True, tb=tb, qb=qb))
        tb += SS
        qb += QS
        s.NTOK = tb
        s.NQ = qb


class Buf:
    def __init__(s, name, dsem=None, multi=False, psum=False):
        s.name = name
        s.psum = psum
        s.wev = {}
        s.rev = {}
        s.dsem = dsem
        s.multi = multi


class Tile:
    def __init__(s, t, b):
        s.t = t
        s.b = b

    def __getitem__(s, k):
        return s.t[k]


def _freeze(fn):
    if fn is None or fn.__closure__ is None:
        return fn
    cells = []
    for c in fn.__closure__:
        try:
            cells.append(types.CellType(c.cell_contents))
        except ValueError:
            cells.append(c)
    return types.FunctionType(fn.__code__, fn.__globals__, fn.__name__, fn.__defaults__, tuple(cells))


class Prog:
    def __init__(s, nc, gstack):
        s.nc = nc
        s.gstack = gstack
        s.semh = {}
        s.cnt = {}
        s.q = {e: [] for e in ENG}
        s.waited = {e: {} for e in ENG}
        s.esem = {e: s.new_sem('es_' + e) for e in ENG}
        s.dpool = {'hw': [s.new_sem('dh%d' % i) for i in range(48)], 'sw': [s.new_sem('dw%d' % i) for i in range(24)]}
        s.dnext = {'hw': 0, 'sw': 0}
        s.stack = None
        s.nins = 0

    def new_sem(s, name):
        h = s.gstack.enter_context(s.nc.semaphore(name))
        k = len(s.semh)
        s.semh[k] = h
        s.cnt[k] = 0
        return k

    def begin_phase(s):
        s.stack = ExitStack()
        s.phase = getattr(s, 'phase', -1) + 1
        s.dnext = {'hw': 0, 'sw': 0}
        s.q = {e: [] for e in ENG}

    def take_dsem(s, kind):
        k = s.dpool[kind][s.dnext[kind]]
        s.dnext[kind] += 1
        return k

    def sb(s, name, shape, dt, dma=False):
        name = 'f%d_%s' % (s.phase, name)
        t = s.stack.enter_context(s.nc.sbuf_tensor(name, list(shape), dt))
        return Tile(t, Buf(name, dsem=s.take_dsem('hw' if dma is True else dma) if dma else None))

    def ps(s, name, shape, dt):
        name = 'f%d_%s' % (s.phase, name)
        t = s.stack.enter_context(s.nc.psum_tensor(name, list(shape), dt))
        return Tile(t, Buf(name, psum=True))

    def op(s, eng, fn, reads=(), writes=(), dsem=None, ninc=1):
        need = {}

        def add(d):
            for k, v in d.items():
                if v > need.get(k, 0):
                    need[k] = v
        for b in reads:
            add(b.wev)
            if b.psum:
                add(b.rev)
        for b in writes:
            if not b.multi:
                add(b.wev)
            add(b.rev)
        own = s.esem[eng]
        waits = []
        for k, v in need.items():
            if k == own and eng == 'pe' and dsem is None:
                continue
            if s.waited[eng].get(k, 0) >= v:
                continue
            s.waited[eng][k] = v
            waits.append((k, v))
        if dsem is None:
            k = own
            s.cnt[k] += 1
            inc = 1
        else:
            k = dsem
            s.cnt[k] += 16 * ninc
            inc = 16
        v = s.cnt[k]
        s.q[eng].append((waits, _freeze(fn), k, inc))
        s.nins += 1
        for b in reads:
            if v > b.rev.get(k, 0):
                b.rev[k] = v
        for b in writes:
            if b.multi:
                if v > b.wev.get(k, 0):
                    b.wev[k] = v
            else:
                b.wev = {k: v}
                b.rev = {}

    def dma(s, eng, out, in_, reads, writes, sbuf_side):
        s.op(eng, lambda e: e.dma_start(out=out, in_=in_), reads=reads, writes=writes, dsem=sbuf_side.dsem)

    def dmas(s, eng, pairs, reads, writes, sbuf_side):
        s.op(eng, lambda e: [e.dma_start(out=o, in_=i) for (o, i) in pairs], reads=reads, writes=writes,
             dsem=sbuf_side.dsem, ninc=len(pairs))

    def end_phase(s):
        for e in ENG:
            waits = []
            for k, v in s.cnt.items():
                if v > 0 and s.waited[e].get(k, 0) < v:
                    s.waited[e][k] = v
                    waits.append((k, v))
            s.q[e].append((waits, None, None, None))
        with s.nc.Block() as blk:
            for eng, dec in [('sp', blk.sync), ('pe', blk.tensor), ('act', blk.scalar), ('dve', blk.vector),
                             ('pool', blk.gpsimd)]:
                items = s.q[eng]

                def body(e, items=items):
                    for waits, fn, k, inc in items:
                        for (wk, wv) in waits:
                            e.wait_ge(s.semh[wk], wv)
                        if fn is None:
                            continue
                        r = fn(e)
                        if not isinstance(r, (list, tuple)):
                            r = [r]
                        for ins in r:
                            ins.then_inc(s.semh[k], inc)
                dec(body)
        s.stack.close()
        s.stack = None


class Ring:
    def __init__(s, P, plan, n=NRING):
        s.P = P
        s.plan = plan
        s.n = min(n, max(1, len(plan)))
        s.slots = [P.sb('ring%d' % i, [128, 1024], BF16, dma=True) for i in range(s.n)]
        s.loaded = 0
        s.used = 0
        s.released = 0
        for _ in range(s.n):
            s._load()

    def _load(s):
        if s.loaded >= len(s.plan):
            return
        key, src, ncol = s.plan[s.loaded]
        sl = s.slots[s.loaded % s.n]
        s.P.dma('sp', sl.t[:, 0:ncol], src, reads=[], writes=[sl.b], sbuf_side=sl.b)
        s.loaded += 1

    def next(s, key):
        k, src, ncol = s.plan[s.used]
        assert k == key, (k, key)
        assert s.used < s.loaded
        sl = s.slots[s.used % s.n]
        s.used += 1
        return sl

    def release(s, cnt=1):
        for _ in range(cnt):
            s.released += 1
            assert s.released <= s.used
            s._load()


def build(cfg, debug=False):
    nc = bass.Bass("TRN2", target_bir_lowering=False)
    NTOK, NQ = cfg.NTOK, cfg.NQ
    okind = "ExternalOutput" if debug else "Internal"

    def din(name, shape, dt=F32):
        return nc.dram_tensor(name, list(shape), dt, kind="ExternalInput").ap()

    def dscr(name, shape, dt):
        return nc.dram_tensor(name, list(shape), dt, kind=okind).ap()

    I = {}
    I['x_all'] = din('x_all', [NTOK, D])
    I['p_own'] = din('p_own', [NQ, PLE])
    I['ropeC'] = din('ropeC', [96, NTOK])
    I['ropeS'] = din('ropeS', [96, NTOK])
    I['edge'] = din('edge', [128, 2])
    I['gcols'] = din('gcols', [128, 37])
    I['gpost'] = din('gpost', [128, 4 * D])
    I['rb_ext'] = din('rb_ext', [33, 8])
    I['onehot'] = din('onehot', [33, 3 * 128 * 128])
    I['sinkb'] = din('sinkb', [128, 8])
    I['ident'] = din('ident', [128, 128], BF16)
    I['shift'] = din('shift', [128, 64])
    for n in ['ffn1', 'ffn2']:
        I[n + '_w1'] = din(n + '_w1', [D, DFF])
        I[n + '_w3'] = din(n + '_w3', [D, DFF])
        I[n + '_w2'] = din(n + '_w2', [DFF, D])
    I['w_in'] = din('w_in', [D, 29 * 128])
    I['w_uq'] = din('w_uq', [384, 16 * 128])
    I['w_uk'] = din('w_uk', [256, 512])
    I['w_uv'] = din('w_uv', [256, 512])
    I['w_pa'] = din('w_pa', [512, D])
    I['w_pb'] = din('w_pb', [512, D])
    I['w_out'] = din('w_out', [D, D])
    I['w_pg'] = din('w_pg', [D, D])
    I['w_pe'] = din('w_pe', [PLE, D])
    y_out = nc.dram_tensor('y_own', [NQ, D], F32, kind="ExternalOutput").ap()

    S = {}
    S['wffn1'] = dscr('wffn1', [66, 128, 1024], BF16)
    S['wffn2'] = dscr('wffn2', [66, 128, 1024], BF16)
    S['win'] = dscr('win', [29, 128, 1024], BF16)
    S['wuq'] = dscr('wuq', [16, 128, 384], BF16)
    S['wukv'] = dscr('wukv', [2, 128, 1024], BF16)
    S['wpab'] = dscr('wpab', [2, 64, 8 * 1024], BF16)
    S['wout'] = dscr('wout', [8, 128, 1024], BF16)
    S['wpg'] = dscr('wpg', [8, 128, 1024], BF16)
    S['wpe'] = dscr('wpe', [2, 128, 1024], BF16)
    S['biasD'] = dscr('biasD', [8, 3 * 128 * 128], F32)
    S['hsp'] = dscr('hsp', [NQ, D], F32)
    S['h2sp'] = dscr('h2sp', [NQ, D], F32)
    S['ckvnT'] = dscr('ckvnT', [256, NTOK], BF16)
    S['kropeT'] = dscr('kropeT', [32, NTOK], BF16)
    S['kbT'] = dscr('kbT', [2, 64, NTOK], BF16)
    S['vbs'] = dscr('vbs', [NTOK, 128], BF16)
    S['QT'] = dscr('QT', [8, 96, NQ], BF16)
    S['qbT'] = dscr('qbT', [8, 64, NQ], BF16)
    S['sga'] = dscr('sga', [8, 128, NQ], BF16)
    S['sgb'] = dscr('sgb', [8, 128, NQ], BF16)
    S['yaT'] = dscr('yaT', [8, 64, NQ], BF16)
    S['ybT'] = dscr('ybT', [8, 64, NQ], BF16)
    DB = {k: Buf('d_' + k, multi=True) for k in list(S.keys()) + ['y_out']}

    gstack = ExitStack()
    P = Prog(nc, gstack)

    phase0(nc, P, cfg, I, S, DB)
    if getattr(cfg, 'stop_after', 9) >= 1:
        phase1(nc, P, cfg, I, S, DB)
    if getattr(cfg, 'stop_after', 9) >= 2:
        phase2a(nc, P, cfg, I, S, DB)
        phase2b(nc, P, cfg, I, S, DB)
    if getattr(cfg, 'stop_after', 9) >= 3:
        phase2c(nc, P, cfg, I, S, DB)
        phase3(nc, P, cfg, I, S, DB, y_out)
    gstack.close()
    return nc


def phase0(nc, P, cfg, I, S, DB):
    P.begin_phase()
    gc = P.sb('gc', [128, 37], F32, dma=True)
    P.dma('sp', gc[:, :], I['gcols'], [], [gc.b], gc.b)
    NB = 4
    stg = [P.sb('stg%d' % i, [128, 8, 512], F32, dma=True) for i in range(NB)]
    cvt = [P.sb('cvt%d' % i, [128, 8, 512], BF16, dma='sw') for i in range(NB)]
    state = dict(i=0)
    engs = ['dve', 'act']

    def conv(src, C, W, dsts, gcol0, st_view=None):
        i = state['i']
        state['i'] += 1
        st = stg[i % NB]
        cv = cvt[i % NB]
        eng = engs[i % 2]
        P.dma('sp', st.t[:, 0:C, 0:W] if st_view is None else st_view(st.t), src, [], [st.b], st.b)
        if gcol0 is None:
            if eng == 'act':
                P.op('act', lambda e: e.activation(out=cv.t[:, 0:C, 0:W], in_=st.t[:, 0:C, 0:W], func=AF.Copy),
                     [st.b], [cv.b])
            else:
                P.op(eng, lambda e: e.tensor_copy(out=cv.t[:, 0:C, 0:W], in_=st.t[:, 0:C, 0:W]), [st.b], [cv.b])
        else:
            for c in range(C):
                sc = gc.t[:, gcol0 + c:gcol0 + c + 1]
                if eng == 'act':
                    P.op('act', lambda e, c=c, sc=sc: e.activation(out=cv.t[:, c, 0:W], in_=st.t[:, c, 0:W],
                                                                     func=AF.Copy, scale=sc), [st.b, gc.b], [cv.b])
                else:
                    P.op(eng, lambda e, c=c, sc=sc: e.tensor_scalar(out=cv.t[:, c, 0:W], in0=st.t[:, c, 0:W],
                                                                      scalar1=sc, scalar2=None, op0=ALU.mult),
                         [st.b, gc.b], [cv.b])
        pairs = [(d, f(cv.t)) for (d, f) in dsts]
        P.dmas('pool', pairs, [cv.b], [], cv.b)

    def conv_U(w, K, ncols, dst, slot_of, gcol0, Wp=512):
        C = K // 128
        wv = w.rearrange("(c p) n -> p c n", p=128)
        for j0 in range(0, ncols, Wp):
            Wc = min(Wp, ncols - j0)
            dsts = []
            for jj in range(Wc // 128):
                sl = slot_of((j0 // 128) + jj)
                d = dst[sl][:, 0:C * 128].rearrange("p (c n) -> p c n", c=C)
                dsts.append((d, lambda t, jj=jj, C=C: t[:, 0:C, jj * 128:(jj + 1) * 128]))
            conv(wv[:, :, j0:j0 + Wc], C, Wc, dsts, gcol0)

    def conv_D(w, K, dst, slot0, gcol0):
        Fn = K // 128
        wv = w.rearrange("(f p) n -> p f n", p=128)
        for f0 in range(0, Fn, 4):
            fc = min(4, Fn - f0)
            src = wv[:, f0:f0 + fc, :].rearrange("p f (h n) -> p f h n", h=2)
            dsts = []
            for ff in range(fc):
                d = dst[slot0 + f0 + ff].rearrange("p (h n) -> p h n", h=2)
                dsts.append((d, lambda t, ff=ff: t[:, 2 * ff:2 * ff + 2, :]))
            conv(src, fc * 2, 512, dsts, None,
                 st_view=lambda t, fc=fc: t[:, 0:2 * fc, :].rearrange("p (f h) n -> p f h n", h=2))

    G_F1, G_MIX, G_Q, G_KV, G_F2, G_PLE = 0, 8, 16, 19, 21, 29
    for n, g0 in [('ffn1', G_F1), ('ffn2', G_F2)]:
        dst = S['w' + n]
        conv_U(I[n + '_w1'], D, DFF, dst, lambda j: 2 * j, g0)
        conv_U(I[n + '_w3'], D, DFF, dst, lambda j: 2 * j + 1, g0)
        conv_D(I[n + '_w2'], DFF, dst, 44, None)
    conv_U(I['w_in'], D, 29 * 128, S['win'], lambda j: j, G_MIX)
    conv_U(I['w_uq'], 384, 16 * 128, S['wuq'], lambda j: j, G_Q)
    for i, nm in enumerate(['w_uk', 'w_uv']):
        wv = I[nm].rearrange("(c p) n -> p c n", p=128)
        d = S['wukv'][i].rearrange("p (c n) -> p c n", c=2)
        conv(wv, 2, 512, [(d, lambda t: t[:, 0:2, :])], G_KV)
    for i, nm in enumerate(['w_pa', 'w_pb']):
        wv = I[nm].rearrange("(h p) n -> p h n", p=64)
        dd = S['wpab'][i].rearrange("p (h n) -> p h n", h=8)
        for hh in range(0, 8, 4):
            src = wv[:, hh:hh + 4, :].rearrange("p h (a n) -> p h a n", a=2)
            i0 = state['i']
            state['i'] += 1
            st = stg[i0 % NB]
            cv = cvt[i0 % NB]
            P.dma('sp', st.t[0:64, :, :].rearrange("p (h a) n -> p h a n", a=2), src, [], [st.b], st.b)
            P.op('dve', lambda e, st=st, cv=cv: e.tensor_copy(out=cv.t[0:64, :, :], in_=st.t[0:64, :, :]), [st.b],
                 [cv.b])
            d = dd[:, hh:hh + 4, :].rearrange("p h (a n) -> p h a n", a=2)
            P.dmas('pool', [(d, cv.t[0:64, :, :].rearrange("p (h a) n -> p h a n", a=2))], [cv.b], [], cv.b)
    conv_D(I['w_out'], D, S['wout'], 0, None)
    wv = I['w_pg'].rearrange("(f p) n -> p f n", p=128)
    for f0 in range(0, 8, 4):
        i0 = state['i']
        state['i'] += 1
        st = stg[i0 % NB]
        cv = cvt[i0 % NB]
        src = wv[:, f0:f0 + 4, :].rearrange("p f (h n) -> p f h n", h=2)
        P.dma('sp', st.t[:, :, :].rearrange("p (f h) n -> p f h n", h=2), src, [], [st.b], st.b)
        for ff in range(4):
            sc = gc.t[:, G_PLE + f0 + ff:G_PLE + f0 + ff + 1]
            P.op('dve', lambda e, ff=ff, sc=sc, st=st, cv=cv: e.tensor_scalar(
                out=cv.t[:, 2 * ff:2 * ff + 2, :], in0=st.t[:, 2 * ff:2 * ff + 2, :], scalar1=sc, scalar2=None,
                op0=ALU.mult), [st.b, gc.b], [cv.b])
        pairs = []
        for ff in range(4):
            d = S['wpg'][f0 + ff].rearrange("p (h n) -> p h n", h=2)
            pairs.append((d, cv.t[:, 2 * ff:2 * ff + 2, :]))
        P.dmas('pool', pairs, [cv.b], [], cv.b)
    conv_D(I['w_pe'], PLE, S['wpe'], 0, None)

    rb = P.sb('rb', [33, 8], F32, dma=True)
    P.dma('sp', rb[:, :], I['rb_ext'], [], [rb.b], rb.b)
    pb = [P.ps('pb%d' % i, [128, 512], F32) for i in range(4)]
    NCH = 3 * 128 * 128 // 2048
    oh = [P.sb('oh%d' % i, [33, 2048], F32, dma=True) for i in range(2)]
    bo = [P.sb('bo%d' % i, [8, 2048], F32, dma='sw') for i in range(2)]
    for ch in range(NCH):
        o = oh[ch % 2]
        b_ = bo[ch % 2]
        P.dma('sp', o[:, :], I['onehot'][:, ch * 2048:(ch + 1) * 2048], [], [o.b], o.b)
        for q in range(4):
            pp = pb[q]
            P.op('pe', lambda e, pp=pp, o=o, q=q: e.matmul(pp.t[0:8, :], lhsT=rb.t[:, :], rhs=o.t[:, q * 512:(q + 1) * 512],
                                                            start=True, stop=True), [rb.b, o.b], [pp.b])
            P.op('dve' if q % 2 == 0 else 'act',
                 (lambda e, pp=pp, b_=b_, q=q: e.tensor_copy(out=b_.t[:, q * 512:(q + 1) * 512], in_=pp.t[0:8, :]))
                 if q % 2 == 0 else
                 (lambda e, pp=pp, b_=b_, q=q: e.activation(out=b_.t[:, q * 512:(q + 1) * 512], in_=pp.t[0:8, :],
                                                             func=AF.Copy)),
                 [pp.b], [b_.b])
        P.dmas('pool', [(S['biasD'][:, ch * 2048:(ch + 1) * 2048], b_.t[:, :])], [b_.b], [DB['biasD']], b_.b)
    P.end_phase()


def rstd_from_ss(P, ss, out, n, half, tmp):
    P.op('dve', lambda e: e.tensor_scalar(out=tmp[0], in0=ss[0], scalar1=1.0 / n, scalar2=EPS, op0=ALU.mult,
                                          op1=ALU.add), [ss[1]], [tmp[1]])
    P.op('pool', lambda e: e.tensor_tensor(out=out[0], in0=tmp[0], in1=half[0], op=ALU.pow), [tmp[1], half[1]],
         [out[1]])


def rstd_big(P, ss, out, n, tmp):
    P.op('act', lambda e: e.activation(out=tmp.t[:, :], in_=ss.t[:, :], func=AF.Sqrt, scale=1.0 / n, bias=EPS),
         [ss.b], [tmp.b])
    P.op('dve', lambda e: e.reciprocal(out=out.t[:, :], in_=tmp.t[:, :]), [tmp.b], [out.b])


EARLY = 3


def ffn_plan(wd, f0=0, f1=NF):
    plan = []
    for f in range(f0, f1):
        plan.append((('u', f, 0), wd[2 * f], 1024))
        plan.append((('u', f, 1), wd[2 * f + 1], 1024))
    return plan


class FfnCtx:
    def __init__(s, nc, P, gpost_ap, ident_ap, wd, nhout=2):
        s.nc, s.P = nc, P
        s.w2 = P.sb('w2res', [128, NF, D], BF16, dma=True)
        w2src = wd[44:66].rearrange("f p n -> p f n")
        P.dmas('sp', [(s.w2.t[:, f0:min(NF, f0 + 6), :], w2src[:, f0:min(NF, f0 + 6), :]) for f0 in range(0, NF, 6)],
               [], [s.w2.b], s.w2.b)
        s.xin = [P.sb('xin%d' % i, [128, 4, D], F32, dma=True) for i in range(2)]
        s.xnb = [P.sb('xnb%d' % i, [128, D], BF16) for i in range(2)]
        s.xnbx = [P.sb('xnbx%d' % i, [128, D], BF16) for i in range(2)]
        s.xnT = P.sb('xnT', [128, 8, 512], BF16)
        s.actT = P.sb('actT', [128, NF, 512], BF16, dma='sw')
        s.sil = [P.sb('sil%d' % i, [128, 512], BF16) for i in range(2)]
        s.hout = [P.sb('hout%d' % i, [128, D], F32, dma='sw') for i in range(nhout)]
        s.nhout = nhout
        s.junk = P.sb('junk', [128, D], BF16)
        s.junk2 = s.junk
        s.stat = [P.sb('stat%d' % i, [128, 8], F32) for i in range(8)]
        s.mhalf = P.sb('mhalf', [128, 512], F32)
        s.gp = P.sb('gp', [128, D], F32, dma=True)
        s.ident = P.sb('identb', [128, 128], BF16, dma=True)
        pall = P.ps('pAll', [128, 4, 512], F32)
        s.pA = [Tile(pall.t[:, i, :], Buf('pA%d' % i, psum=True)) for i in range(4)]
        s.pY = P.ps('pY', [128, 2, 512], F32)
        s.pYs = [(s.pY.t[:, :, :], [s.pY.b]), (pall.t[:, 2:4, :], [s.pA[2].b, s.pA[3].b])]
        s.alim = 4
        s.pT = [P.ps('pT%d' % i, [128, D], BF16) for i in range(2)]
        s.ai = 0
        s.si = 0
        s.ti = 0
        s.early = 0
        P.dma('sp', s.gp[:, :], gpost_ap, [], [s.gp.b], s.gp.b)
        P.dma('sp', s.ident[:, :], ident_ap, [], [s.ident.b], s.ident.b)
        P.op('pool', lambda e: e.memset(s.mhalf.t[:, :], -0.5), [], [s.mhalf.b])
        P.op('dve', lambda e: e.tensor_scalar(out=s.gp.t[:, :], in0=s.gp.t[:, :], scalar1=0.5, scalar2=None,
                                              op0=ALU.mult), [s.gp.b], [s.gp.b])

    def nextA(s):
        t = s.pA[s.ai % s.alim]
        s.ai += 1
        return t

    def nstat(s):
        t = s.stat[s.si % 8]
        s.si += 1
        return t

    def norm_pre(s, src_ap, src_b, xb):
        P = s.P
        st = s.nstat()
        jk = s.junk
        P.op('act', lambda e: e.activation(out=jk.t[:, :], in_=src_ap, func=AF.Square, accum_out=st.t[:, 0:1]),
             [src_b], [jk.b, st.b])
        rstd_from_ss(P, (st.t[:, 0:1], st.b), (st.t[:, 2:3], st.b), D, (s.mhalf.t[:, 0:1], s.mhalf.b),
                     (st.t[:, 1:2], st.b))
        P.op('dve', lambda e: e.tensor_scalar(out=xb.t[:, :], in0=src_ap, scalar1=st.t[:, 2:3], scalar2=None,
                                              op0=ALU.mult), [src_b, st.b], [xb.b])

    def norm_tr(s, xb, dstT, sub, pt):
        P = s.P
        for c in range(8):
            P.op('pe', lambda e, c=c: e.transpose(out=pt.t[:, c * 128:(c + 1) * 128], in_=xb.t[:, c * 128:(c + 1) * 128],
                                                  identity=s.ident.t[:, :]), [xb.b, s.ident.b], [pt.b])
        P.op('dve', lambda e: e.tensor_copy(out=dstT.t[:, :, sub * 128:(sub + 1) * 128],
                                            in_=pt.t[:, :].rearrange("p (c n) -> p c n", c=8)), [pt.b], [dstT.b])

    def load_x(s, x_src, buf):
        xi = s.xin[buf]
        s.P.dma('sp', xi.t[:, :, :], x_src.rearrange("(s p) d -> p s d", p=128), [], [xi.b], xi.b)

    def first(s, x_src):
        s.load_x(x_src, 0)
        xi = s.xin[0]
        for sub in range(4):
            xb = s.xnbx[sub % 2]
            s.norm_pre(xi.t[:, sub, :], xi.b, xb)
            s.norm_tr(xb, s.xnT, sub, s.pT[1])

    def run_tile(s, ring, next_src, cb_pre, cb_T, cb_mm=None):
        P = s.P
        ti = s.ti
        s.ti += 1
        xi = s.xin[ti % 2]
        xn = s.xin[(ti + 1) % 2]
        xT = s.xnT
        if next_src is not None:
            s.load_x(next_src, (ti + 1) % 2)
        def up(f):
            w1 = ring.next(('u', f, 0))
            w3 = ring.next(('u', f, 1))
            h1 = s.nextA()
            h3 = s.nextA()
            for (w, h) in [(w1, h1), (w3, h3)]:
                for c in range(8):
                    P.op('pe', lambda e, w=w, h=h, c=c: e.matmul(h.t[:, :], lhsT=w.t[:, c * 128:(c + 1) * 128],
                                                                  rhs=xT.t[:, c, :], start=(c == 0), stop=(c == 7)),
                         [w.b, xT.b], [h.b])
            ring.release(2)
            sl = s.sil[f % 2]
            P.op('act', lambda e, h1=h1, sl=sl: e.activation(out=sl.t[:, :], in_=h1.t[:, :], func=AF.Silu), [h1.b],
                 [sl.b])
            P.op('dve', lambda e, h3=h3, sl=sl, f=f: e.tensor_tensor(out=s.actT.t[:, f, :], in0=sl.t[:, :],
                                                                    in1=h3.t[:, :], op=ALU.mult), [sl.b, h3.b],
                 [s.actT.b])
        for f in range(s.early, NF):
            up(f)
        s.early = 0
        xbs = {}
        s.alim = 2
        for sub in range(4):
            py, pyb = s.pYs[sub % 2]
            if next_src is not None:
                xbs[sub] = s.xnbx[sub % 2]
                s.norm_pre(xn.t[:, sub, :], xn.b, xbs[sub])
            for half in range(2):
                for f in range(NF):
                    P.op('pe', lambda e, f=f, half=half, sub=sub: e.matmul(
                        py[:, half, :], lhsT=s.actT.t[:, f, sub * 128:(sub + 1) * 128],
                        rhs=s.w2.t[:, f, half * 512:(half + 1) * 512], start=(f == 0), stop=(f == NF - 1)),
                        [s.actT.b, s.w2.b], pyb)
            if sub >= 1:
                cb_T(sub - 1)
                if next_src is not None:
                    s.norm_tr(xbs[sub - 1], xT, sub - 1, s.pT[1])
            st = s.nstat()
            jk = s.junk2
            P.op('act', lambda e, st=st: e.activation(out=jk.t[:, :].rearrange("p (a n) -> p a n", a=2), in_=py,
                                                      func=AF.Square, accum_out=st.t[:, 0:1]), pyb, [jk.b, st.b])
            rstd_from_ss(P, (st.t[:, 0:1], st.b), (st.t[:, 2:3], st.b), D, (s.mhalf.t[:, 0:1], s.mhalf.b),
                         (st.t[:, 1:2], st.b))
            ho = s.hout[sub % s.nhout]
            P.op('dve', lambda e, st=st, ho=ho: e.scalar_tensor_tensor(
                out=ho.t[:, :].rearrange("p (a n) -> p a n", a=2), in0=py, scalar=st.t[:, 2:3],
                in1=s.gp.t[:, :].rearrange("p (a n) -> p a n", a=2), op0=ALU.mult, op1=ALU.mult),
                pyb + [st.b, s.gp.b], [ho.b])
            P.op('dve', lambda e, ho=ho, sub=sub: e.tensor_tensor(out=ho.t[:, :], in0=ho.t[:, :],
                                                                 in1=xi.t[:, sub, :], op=ALU.add),
                 [ho.b, xi.b], [ho.b])
            cb_pre(sub, ho)
            if sub >= 2 and cb_mm is not None:
                cb_mm(sub - 2)
        if next_src is not None:
            s.norm_tr(xbs[3], xT, 3, s.pT[1])
            for f in range(EARLY):
                up(f)
            s.early = EARLY
        cb_T(3)
        if cb_mm is not None:
            cb_mm(2)
            cb_mm(3)
        s.alim = 4


def phase1(nc, P, cfg, I, S, DB):
    P.begin_phase()
    tiles = []
    for j in cfg.jobs:
        for t0 in range(0, j['ntok'], 512):
            own = (t0 >= j['q0']) and (t0 < j['q0'] + j['nq'])
            tiles.append((j, t0, own))
    plan = []
    for ti_, (j, t0, own) in enumerate(tiles):
        plan += ffn_plan(S['wffn1'], EARLY if ti_ > 0 else 0, NF)
        if ti_ + 1 < len(tiles):
            plan += ffn_plan(S['wffn1'], 0, EARLY)
        nsl = 29 if own else 6
        for k in range(nsl):
            plan.append((('in', k), S['win'][k], 1024))
        if own:
            for k in range(16):
                plan.append((('uq', k), S['wuq'][k], 384))
    ring = Ring(P, plan, n=12)
    fc = FfnCtx(nc, P, I['gpost'][:, 0:D], I['ident'], S['wffn1'])
    j0, t00, _ = tiles[0]
    fc.first(I['x_all'][j0['tb'] + t00:j0['tb'] + t00 + 512, :])
    uT = [P.sb('uT%d' % i, [128, 8, 512], BF16) for i in range(1)]
    ones = P.sb('onesb', [128, 128], BF16)
    P.op('pool', lambda e: e.memset(ones.t[:, :], 1.0), [], [ones.b])
    cqT = P.sb('cqT', [128, 3, 512], BF16)
    sq = [P.sb('sq%d' % i, [128, 512], BF16) for i in range(3)]
    ckvT = P.sb('ckvT', [128, 2, 512], F32)
    rbc = [P.sb('rbc%d' % i, [128, 512], F32) for i in range(2)]
    rtmp = P.sb('rtmp', [128, 512], F32)
    tabC = P.sb('tabC', [96, 512], F32, dma=True)
    tabS = P.sb('tabS', [96, 512], F32, dma=True)
    CR = P.sb('CR', [96, 512], F32)
    SR = P.sb('SR', [96, 512], F32)
    t1 = [P.sb('t1_%d' % i, [96, 512], F32) for i in range(1)]
    t2 = [P.sb('t2_%d' % i, [96, 512], F32) for i in range(1)]
    qst = [P.sb('qst%d' % i, [96, 512], BF16, dma='sw') for i in range(3)]
    ckvn = P.sb('ckvn', [128, 2, 512], BF16, dma='sw')
    krs_sb = P.sb('krs_sb', [96, 512], F32)
    kro = P.sb('kro', [96, 512], BF16, dma='sw')
    kbo = P.sb('kbo', [64, 2, 512], BF16, dma='sw')
    vbo = P.sb('vbo', [128, 4, 128], BF16, dma='sw')
    qbo = Tile(fc.actT.t[0:64, 14:22, :], fc.actT.b)
    sgo = [Tile(fc.actT.t[:, 6:14, :], fc.actT.b)]
    QSCALE = 96.0 ** -0.5

    for ti, (j, t0, own) in enumerate(tiles):
        g0 = j['tb'] + t0
        nsrc = None
        if ti + 1 < len(tiles):
            jn, tn, _ = tiles[ti + 1]
            nsrc = I['x_all'][jn['tb'] + tn:jn['tb'] + tn + 512, :]
        u = uT[0]
        uxb = {}

        def cb_pre(sub, ho, j=j, t0=t0, own=own):
            if own:
                q0g_ = j['qb'] + (t0 - j['q0']) + sub * 128
                P.dma('pool', S['hsp'][q0g_:q0g_ + 128, :], ho.t[:, :], [ho.b], [DB['hsp']], ho.b)
            uxb[sub] = fc.xnb[sub % 2]
            fc.norm_pre(ho.t[:, :], ho.b, uxb[sub])

        def cb_T(sub, u=u):
            fc.norm_tr(uxb[sub], u, sub, fc.pT[0])
        fc.run_tile(ring, nsrc, cb_pre, cb_T)
        CUT = getattr(cfg, 'cut', 99)

        def drain(kfrom, own=own):
            for k in range(kfrom, 29 if own else 6):
                ring.next(('in', k))
                ring.release(1)
            if own:
                for k in range(16):
                    ring.next(('uq', k))
                    ring.release(1)
        if CUT <= 1:
            drain(0)
            continue
        P.dma('sp', tabC[:, :], I['ropeC'][:, g0:g0 + 512], [], [tabC.b], tabC.b)
        P.dma('sp', tabS[:, :], I['ropeS'][:, g0:g0 + 512], [], [tabS.b], tabS.b)

        def lin(key, ncols, col0=0):
            w = ring.next(key)
            pp = fc.nextA()
            for c in range(8):
                P.op('pe', lambda e, w=w, pp=pp, c=c: e.matmul(pp.t[0:ncols, :],
                                                                lhsT=w.t[:, c * 128 + col0:c * 128 + col0 + ncols],
                                                                rhs=u.t[:, c, :], start=(c == 0), stop=(c == 7)),
                     [w.b, u.b], [pp.b])
            return w, pp

        for c2 in range(2):
            w, pp = lin(('in', c2), 128)
            ring.release(1)
            P.op('dve', lambda e, pp=pp, c2=c2: e.tensor_copy(out=ckvT.t[:, c2, :], in_=pp.t[:, :]), [pp.b], [ckvT.b])
            P.op('act', lambda e, pp=pp, c2=c2: e.activation(out=sq[c2].t[:, :], in_=pp.t[:, :], func=AF.Square),
                 [pp.b], [sq[c2].b])
        pss = fc.nextA()
        for c2 in range(2):
            P.op('pe', lambda e, c2=c2: e.matmul(pss.t[:, :], lhsT=ones.t[:, :], rhs=sq[c2].t[:, :], start=(c2 == 0),
                                                 stop=(c2 == 1)), [ones.b, sq[c2].b], [pss.b])
        rk = rbc[0]
        rstd_big(P, pss, rk, 256, rtmp)
        for c2 in range(2):
            P.op('dve', lambda e, c2=c2: e.tensor_tensor(out=ckvn.t[:, c2, :], in0=ckvT.t[:, c2, :], in1=rk.t[:, :],
                                                         op=ALU.mult), [ckvT.b, rk.b], [ckvn.b])
        P.dma('pool', S['ckvnT'].rearrange("(c p) n -> p c n", p=128)[:, :, g0:g0 + 512], ckvn.t[:, :, :],
              [ckvn.b], [DB['ckvnT']], ckvn.b)
        if CUT <= 2:
            drain(2)
            continue
        w, pk = lin(('in', 2), 96)
        ring.release(1)
        w, pks = lin(('in', 3), 96)
        ring.release(1)
        P.op('dve', lambda e: e.tensor_tensor(out=krs_sb.t[64:96, :], in0=pks.t[64:96, :], in1=tabS.t[64:96, :],
                                              op=ALU.mult), [pks.b, tabS.b], [krs_sb.b])
        P.op('dve', lambda e: e.tensor_tensor(out=t1[0].t[64:96, :], in0=pk.t[64:96, :], in1=tabC.t[64:96, :],
                                              op=ALU.mult), [pk.b, tabC.b], [t1[0].b])
        P.op('pool', lambda e: e.tensor_tensor(out=kro.t[64:96, :], in0=t1[0].t[64:96, :], in1=krs_sb.t[64:96, :],
                                               op=ALU.add), [t1[0].b, krs_sb.b], [kro.b])
        P.dma('pool', S['kropeT'][:, g0:g0 + 512], kro.t[64:96, :], [kro.b], [DB['kropeT']], kro.b)
        if CUT <= 3:
            drain(4)
            continue
        w = ring.next(('in', 4))
        for g in range(2):
            pp = fc.nextA()
            for c in range(8):
                P.op('pe', lambda e, w=w, pp=pp, c=c, g=g: e.matmul(pp.t[0:64, :],
                                                                    lhsT=w.t[:, c * 128 + g * 64:c * 128 + g * 64 + 64],
                                                                    rhs=u.t[:, c, :], start=(c == 0), stop=(c == 7)),
                     [w.b, u.b], [pp.b])
            P.op('act', lambda e, pp=pp, g=g: e.activation(out=kbo.t[:, g, :], in_=pp.t[0:64, :], func=AF.Copy),
                 [pp.b], [kbo.b])
        ring.release(1)
        P.dma('pool', S['kbT'].rearrange("g p n -> p g n")[:, :, g0:g0 + 512], kbo.t[:, :, :], [kbo.b], [DB['kbT']],
              kbo.b)
        w = ring.next(('in', 5))
        pv = fc.nextA()
        for sub in range(4):
            for c in range(8):
                P.op('pe', lambda e, w=w, sub=sub, c=c: e.matmul(pv.t[:, sub * 128:(sub + 1) * 128],
                                                                 lhsT=u.t[:, c, sub * 128:(sub + 1) * 128],
                                                                 rhs=w.t[:, c * 128:(c + 1) * 128], start=(c == 0),
                                                                 stop=(c == 7)), [w.b, u.b], [pv.b])
        ring.release(1)
        P.op('dve', lambda e: e.tensor_copy(out=vbo.t[:, :, :], in_=pv.t[:, :].rearrange("p (s n) -> p s n", s=4)),
             [pv.b], [vbo.b])
        P.dma('pool', S['vbs'][g0:g0 + 512, :].rearrange("(s p) n -> p s n", p=128), vbo.t[:, :, :], [vbo.b],
              [DB['vbs']], vbo.b)
        if not own:
            continue
        if CUT <= 4:
            drain(6)
            continue
        q0g = j['qb'] + (t0 - j['q0'])
        for c3 in range(3):
            w, pp = lin(('in', 6 + c3), 128)
            ring.release(1)
            P.op('dve', lambda e, pp=pp, c3=c3: e.tensor_copy(out=cqT.t[:, c3, :], in_=pp.t[:, :]), [pp.b], [cqT.b])
            P.op('act', lambda e, pp=pp, c3=c3: e.activation(out=sq[c3].t[:, :], in_=pp.t[:, :], func=AF.Square),
                 [pp.b], [sq[c3].b])
        pss = fc.nextA()
        for c3 in range(3):
            P.op('pe', lambda e, c3=c3: e.matmul(pss.t[:, :], lhsT=ones.t[:, :], rhs=sq[c3].t[:, :], start=(c3 == 0),
                                                 stop=(c3 == 2)), [ones.b, sq[c3].b], [pss.b])
        rq = rbc[1]
        rstd_big(P, pss, rq, 384, rtmp)
        P.op('dve', lambda e: e.scalar_tensor_tensor(out=CR.t[:, :], in0=tabC.t[:, :], scalar=QSCALE,
                                                     in1=rq.t[0:96, :], op0=ALU.mult, op1=ALU.mult),
             [tabC.b, rq.b], [CR.b])
        P.op('dve', lambda e: e.scalar_tensor_tensor(out=SR.t[:, :], in0=tabS.t[:, :], scalar=QSCALE,
                                                     in1=rq.t[0:96, :], op0=ALU.mult, op1=ALU.mult),
             [tabS.b, rq.b], [SR.b])
        if CUT <= 5:
            drain(9)
            continue
        for k in range(4):
            w = ring.next(('in', 9 + k))
            for hh in range(2):
                h = 2 * k + hh
                pp = fc.nextA()
                for c in range(8):
                    P.op('pe', lambda e, w=w, pp=pp, c=c, hh=hh: e.matmul(
                        pp.t[0:64, :], lhsT=w.t[:, c * 128 + hh * 64:c * 128 + hh * 64 + 64], rhs=u.t[:, c, :],
                        start=(c == 0), stop=(c == 7)), [w.b, u.b], [pp.b])
                P.op('act', lambda e, pp=pp, h=h: e.activation(out=qbo.t[:, h, :], in_=pp.t[0:64, :], func=AF.Copy,
                                                               scale=0.125), [pp.b], [qbo.b])
            ring.release(1)
        P.dma('pool', S['qbT'].rearrange("h p n -> p h n")[:, :, q0g:q0g + 512], qbo.t, [qbo.b], [DB['qbT']],
              qbo.b)
        if CUT <= 6:
            drain(13)
            continue
        for gi, nm in enumerate(['sga', 'sgb']):
            so = sgo[0]
            for k in range(8):
                w, pp = lin(('in', 13 + gi * 8 + k), 128)
                ring.release(1)
                P.op('act', lambda e, pp=pp, k=k, so=so: e.activation(out=so.t[:, k, :], in_=pp.t[:, :],
                                                                       func=AF.Sigmoid), [pp.b], [so.b])
            P.dma('pool', S[nm].rearrange("k p n -> p k n")[:, :, q0g:q0g + 512], so.t, [so.b], [DB[nm]],
                  so.b)
        if CUT <= 7:
            drain(29)
            continue
        for h in range(8):
            wr = ring.next(('uq', 2 * h))
            ws = ring.next(('uq', 2 * h + 1))
            pr = fc.nextA()
            pw = fc.nextA()
            for (w, pp) in [(wr, pr), (ws, pw)]:
                for c3 in range(3):
                    P.op('pe', lambda e, w=w, pp=pp, c3=c3: e.matmul(pp.t[0:96, :], lhsT=w.t[:, c3 * 128:c3 * 128 + 96],
                                                                      rhs=cqT.t[:, c3, :], start=(c3 == 0),
                                                                      stop=(c3 == 2)), [w.b, cqT.b], [pp.b])
            ring.release(2)
            a = t1[0]
            b = t2[0]
            P.op('dve', lambda e, pr=pr, a=a: e.tensor_tensor(out=a.t[:, :], in0=pr.t[0:96, :], in1=CR.t[:, :],
                                                              op=ALU.mult), [pr.b, CR.b], [a.b])
            P.op('dve', lambda e, pw=pw, b=b: e.tensor_tensor(out=b.t[:, :], in0=pw.t[0:96, :], in1=SR.t[:, :],
                                                              op=ALU.mult), [pw.b, SR.b], [b.b])
            qs = qst[h % 3]
            P.op('pool', lambda e, a=a, b=b, qs=qs: e.tensor_tensor(out=qs.t[:, :], in0=a.t[:, :], in1=b.t[:, :],
                                                                     op=ALU.add), [a.b, b.b], [qs.b])
            P.dma('pool', S['QT'][h][:, q0g:q0g + 512], qs.t[:, :], [qs.b], [DB['QT']], qs.b)
    P.end_phase()


def phase2a(nc, P, cfg, I, S, DB):
    P.begin_phase()
    maxk = max(j['ntok'] for j in cfg.jobs)
    wuk = P.sb('wuk', [128, 2, 512], BF16, dma=True)
    wuv = P.sb('wuv', [128, 2, 512], BF16, dma=True)
    P.dma('sp', wuk.t[:, :, :], S['wukv'][0].rearrange("p (c n) -> p c n", c=2), [DB['wukv']], [wuk.b], wuk.b)
    P.dma('sp', wuv.t[:, :, :], S['wukv'][1].rearrange("p (c n) -> p c n", c=2), [DB['wukv']], [wuv.b], wuv.b)
    shf = P.sb('shf', [128, 64], F32, dma=True)
    P.dma('sp', shf.t[:, :], I['shift'], [], [shf.b], shf.b)
    KT = [P.sb('KT%d' % i, [96, maxk], BF16, dma=True) for i in range(2)]
    VA = [P.sb('VA%d' % i, [128, maxk // 128, 128], BF16) for i in range(2)]
    for i in range(2):
        P.op('pool', lambda e, i=i: e.memset(VA[i].t[:, :, 64:128], 1.0), [], [VA[i].b])
    ck = [P.sb('ck%d' % i, [128, 2, 512], BF16, dma=True) for i in range(2)]
    qt = [P.sb('qt%d' % i, [96, 512], BF16, dma=True) for i in range(2)]
    PT = [P.sb('PT%d' % i, [128, 1024], BF16) for i in range(3)]
    osb = [P.sb('osb%d' % i, [128, 512], F32) for i in range(2)]
    yst = [P.sb('yst%d' % i, [64, 512], BF16, dma='sw') for i in range(2)]
    pS = [P.ps('pS%d' % i, [128, 1024], F32) for i in range(3)]
    pO = P.ps('pO', [128, 512], F32)
    pB = P.ps('pB', [128, 512], F32)
    cnt = dict(ck=0, qt=0, o=0, y=0)

    def build_kv(j, h, buf):
        kt, va = KT[buf], VA[buf]
        tb, nk = j['tb'], j['ntok']
        for k0 in range(0, nk, 512):
            c = ck[cnt['ck'] % 2]
            cnt['ck'] += 1
            P.dma('sp', c.t[:, :, :], S['ckvnT'].rearrange("(c p) n -> p c n", p=128)[:, :, tb + k0:tb + k0 + 512],
                  [DB['ckvnT']], [c.b], c.b)
            for c2 in range(2):
                P.op('pe', lambda e, c2=c2: e.matmul(pB.t[0:64, :], lhsT=wuk.t[:, c2, h * 64:(h + 1) * 64],
                                                     rhs=c.t[:, c2, :], start=(c2 == 0), stop=(c2 == 1)),
                     [wuk.b, c.b], [pB.b])
            P.op('dve', lambda e: e.tensor_copy(out=kt.t[0:64, k0:k0 + 512], in_=pB.t[0:64, :]), [pB.b], [kt.b])
            yield
            for s4 in range(4):
                for c2 in range(2):
                    P.op('pe', lambda e, c2=c2, s4=s4: e.matmul(pB.t[:, s4 * 64:(s4 + 1) * 64],
                                                                 lhsT=c.t[:, c2, s4 * 128:(s4 + 1) * 128],
                                                                 rhs=wuv.t[:, c2, h * 64:(h + 1) * 64],
                                                                 start=(c2 == 0), stop=(c2 == 1)),
                         [wuv.b, c.b], [pB.b])
            P.op('dve', lambda e: e.tensor_copy(out=va.t[:, k0 // 128:k0 // 128 + 4, 0:64],
                                                in_=pB.t[:, 0:256].rearrange("p (s n) -> p s n", s=4)),
                 [pB.b], [va.b])
            yield

    def drain(gen):
        if gen is not None:
            for _ in gen:
                pass

    for j in cfg.jobs:
        tb, nk, nq, q0, qb = j['tb'], j['ntok'], j['nq'], j['q0'], j['qb']
        for i in range(2):
            P.dma('sp', KT[i].t[64:96, 0:nk], S['kropeT'][:, tb:tb + nk], [DB['kropeT']], [KT[i].b], KT[i].b)
        drain(build_kv(j, 0, 0))
        nsb = nk // 256
        nqt = nq // 512
        stride = max(1, (nsb * nqt - 2) // (2 * (nk // 512)))
        pending = [None]
        for h in range(8):
            kt, va = KT[h % 2], VA[h % 2]
            gen = build_kv(j, h + 1, (h + 1) % 2) if h + 1 < 8 else None
            gcount = 0
            for qi in range(nqt):
                q = qt[cnt['qt'] % 2]
                cnt['qt'] += 1
                P.dma('sp', q.t[:, :], S['QT'][h][:, qb + qi * 512:qb + (qi + 1) * 512], [DB['QT']], [q.b], q.b)
                po = pO

                def qk(sbi):
                    ps_ = pS[sbi % 3]
                    for t in range(2):
                        k0 = (2 * sbi + t) * 128
                        P.op('pe', lambda e, t=t, k0=k0: e.matmul(ps_.t[:, t * 512:(t + 1) * 512],
                                                                  lhsT=kt.t[:, k0:k0 + 128], rhs=q.t[:, :],
                                                                  start=True, stop=True), [kt.b, q.b], [ps_.b])
                    pt_ = PT[sbi % 3]
                    P.op('act', lambda e: e.activation(out=pt_.t[:, :], in_=ps_.t[:, :], func=AF.Exp), [ps_.b], [pt_.b])

                def pv(sbi):
                    pt_ = PT[sbi % 3]
                    for t in range(2):
                        kk = 2 * sbi + t
                        P.op('pe', lambda e, t=t, kk=kk: e.matmul(po.t[:, :], lhsT=va.t[:, kk, :],
                                                                  rhs=pt_.t[:, t * 512:(t + 1) * 512],
                                                                  start=(kk == 0), stop=(kk == 2 * nsb - 1)),
                             [va.b, pt_.b], [po.b])
                qk(0)
                if nsb > 1:
                    qk(1)
                for sbi in range(nsb):
                    if sbi + 2 < nsb:
                        qk(sbi + 2)
                    pv(sbi)
                    if sbi == 1 and pending[0] is not None:
                        pending[0]()
                        pending[0] = None
                    gcount += 1
                    if gen is not None and gcount % stride == 0:
                        next(gen, None)
                ob = osb[cnt['o'] % 2]
                cnt['o'] += 1
                P.op('dve', lambda e: e.tensor_copy(out=ob.t[0:64, :], in_=po.t[0:64, :]), [po.b], [ob.b])
                P.op('dve', lambda e: e.reciprocal(out=ob.t[64:128, :], in_=po.t[64:128, :]), [po.b], [ob.b])

                def part2(ob=ob, h=h, qi=qi, qb=qb):
                    P.op('pe', lambda e: e.matmul(pB.t[0:64, :], lhsT=shf.t[64:128, :], rhs=ob.t[64:128, :],
                                                  start=True, stop=True), [shf.b, ob.b], [pB.b])
                    ys = yst[cnt['y'] % 2]
                    cnt['y'] += 1
                    P.op('dve', lambda e: e.tensor_tensor(out=ys.t[:, :], in0=ob.t[0:64, :], in1=pB.t[0:64, :],
                                                          op=ALU.mult), [ob.b, pB.b], [ys.b])
                    P.dma('pool', S['yaT'][h][:, qb + qi * 512:qb + (qi + 1) * 512], ys.t[:, :], [ys.b], [DB['yaT']],
                          ys.b)
                if pending[0] is not None:
                    pending[0]()
                pending[0] = part2
            drain(gen)
        if pending[0] is not None:
            pending[0]()
            pending[0] = None
    P.end_phase()


def phase2b(nc, P, cfg, I, S, DB):
    P.begin_phase()
    bias = P.sb('bias', [128, 3, 8, 128], F32, dma=True)
    bsrc = S['biasD'].rearrange("h (k j i) -> j k h i", k=3, j=128)
    P.dmas('sp', [(bias.t[:, k, :, :], bsrc[:, k, :, :]) for k in range(3)], [DB['biasD']], [bias.b], bias.b)
    edge = P.sb('edge', [128, 2], F32, dma=True)
    P.dma('sp', edge.t[:, :], I['edge'], [], [edge.b], edge.b)
    snk = P.sb('snk', [128, 8], F32, dma=True)
    P.dma('sp', snk.t[:, :], I['sinkb'], [], [snk.b], snk.b)
    ident = P.sb('identw', [128, 128], BF16, dma=True)
    P.dma('sp', ident.t[:, :], I['ident'], [], [ident.b], ident.b)
    esk = P.sb('esk', [128, 8], F32)
    P.op('act', lambda e: e.activation(out=esk.t[:, :], in_=snk.t[:, :], func=AF.Exp), [snk.b], [esk.b])
    qbt = [P.sb('qbt%d' % i, [64, 8, 512], BF16, dma=True) for i in range(2)]
    kbt = [P.sb('kbt%d' % i, [64, 2, 768], BF16, dma=True) for i in range(2)]
    vbt = [P.sb('vbt%d' % i, [128, 6, 2, 128], BF16, dma=True) for i in range(2)]
    for i in range(2):
        P.op('pool', lambda e, i=i: e.memset(vbt[i].t[:, :, :, 64:128], 1.0), [], [vbt[i].b])
    sbs = [P.sb('sbs%d' % i, [128, 3, 512], F32) for i in range(2)]
    pt3 = [P.sb('pt3_%d' % i, [128, 3, 512], BF16) for i in range(4)]
    den8 = [P.sb('den8_%d' % i, [128, 8], F32) for i in range(2)]
    ytm = [P.sb('ytm%d' % i, [128, 8, 64], BF16) for i in range(2)]
    ybo = [P.sb('ybo%d' % i, [128, 4, 512], BF16, dma='sw') for i in range(2)]
    pS3 = [P.ps('pS3_%d' % i, [128, 3, 512], F32) for i in range(2)]
    pVg = [P.ps('pVw%d' % i, [128, 4, 128], F32) for i in range(2)]
    it = 0
    ybdst = S['ybT'].rearrange("(hp h2) p n -> hp (h2 p) n", h2=2)
    for j in cfg.jobs:
        tb, nk, nq, q0, qb = j['tb'], j['ntok'], j['nq'], j['q0'], j['qb']
        for ti in range(nq // 512):
            qs = q0 + ti * 512
            klo = max(0, qs - 128)
            khi = min(nk, qs + 512 + 128)
            qt_ = qbt[ti % 2]
            kt_ = kbt[ti % 2]
            vt_ = vbt[ti % 2]
            yo = ybo[ti % 2]
            P.dma('sp', qt_.t[:, :, :], S['qbT'].rearrange("h p n -> p h n")[:, :, qb + ti * 512:qb + (ti + 1) * 512],
                  [DB['qbT']], [qt_.b], qt_.b)
            off = klo - (qs - 128)
            P.dma('sp', kt_.t[:, :, off:off + (khi - klo)],
                  S['kbT'].rearrange("g p n -> p g n")[:, :, tb + klo:tb + khi], [DB['kbT']], [kt_.b], kt_.b)
            P.dmas('sp', [(vt_.t[:, off // 128:(off + khi - klo) // 128, g, 0:64],
                           S['vbs'][tb + klo:tb + khi, g * 64:(g + 1) * 64].rearrange("(s p) n -> p s n", p=128))
                          for g in range(2)], [DB['vbs']], [vt_.b], vt_.b)
            def kbs_of(ql):
                qtok = qs + ql * 128
                return [kb for kb in range(3) if 0 <= qtok + (kb - 1) * 128 < nk]

            def stage1(ql):
                kbs = kbs_of(ql)
                k0, k1 = kbs[0], kbs[-1] + 1
                for g in range(2):
                    ps_ = pS3[g]
                    sb_ = sbs[g]
                    p3 = pt3[(2 * ql + g) % 4]
                    for kb in kbs:
                        col = (ql + kb) * 128
                        P.op('pe', lambda e, kb=kb, col=col: e.matmul(
                            ps_.t[:, kb, :], lhsT=kt_.t[:, g, col:col + 128],
                            rhs=qt_.t[:, g * 4:(g + 1) * 4, ql * 128:(ql + 1) * 128], start=True, stop=True),
                            [kt_.b, qt_.b], [ps_.b])
                    P.op('dve', lambda e: e.tensor_tensor(
                        out=sb_.t[:, k0:k1, :].rearrange("p k (h i) -> p k h i", h=4),
                        in0=ps_.t[:, k0:k1, :].rearrange("p k (h i) -> p k h i", h=4),
                        in1=bias.t[:, k0:k1, g * 4:(g + 1) * 4, :], op=ALU.add), [ps_.b, bias.b], [sb_.b])
                    if j['sample'] and ti == 0 and ql == 0:
                        P.op('dve', lambda e: e.tensor_scalar(out=sb_.t[:, 0, :], in0=sb_.t[:, 0, :],
                                                              scalar1=edge.t[:, 0:1], scalar2=None, op0=ALU.add),
                             [sb_.b, edge.b], [sb_.b])
                    if j['sample'] and ti == nq // 512 - 1 and ql == 3:
                        P.op('dve', lambda e: e.tensor_scalar(out=sb_.t[:, 2, :], in0=sb_.t[:, 2, :],
                                                              scalar1=edge.t[:, 1:2], scalar2=None, op0=ALU.add),
                             [sb_.b, edge.b], [sb_.b])
                    P.op('act', lambda e: e.activation(out=p3.t[:, k0:k1, :], in_=sb_.t[:, k0:k1, :], func=AF.Exp),
                         [sb_.b], [p3.b])

            def stage2(ql):
                kbs = kbs_of(ql)
                for g in range(2):
                    p3 = pt3[(2 * ql + g) % 4]
                    pv_ = pVg[g]
                    for hh in range(4):
                        for kb in kbs:
                            P.op('pe', lambda e, kb=kb, hh=hh: e.matmul(
                                pv_.t[:, hh, 0:65], lhsT=p3.t[:, kb, hh * 128:(hh + 1) * 128],
                                rhs=vt_.t[:, ql + kb, g, 0:65], start=(kb == kbs[0]), stop=(kb == kbs[-1])),
                                [p3.b, vt_.b], [pv_.b])
                d8 = den8[ql % 2]
                for g in range(2):
                    P.op('dve', lambda e, g=g: e.tensor_tensor(out=d8.t[:, g * 4:(g + 1) * 4], in0=pVg[g].t[:, :, 64],
                                                               in1=esk.t[:, g * 4:(g + 1) * 4], op=ALU.add),
                         [pVg[g].b, esk.b], [d8.b])
                P.op('dve', lambda e: e.reciprocal(out=d8.t[:, :], in_=d8.t[:, :]), [d8.b], [d8.b])
                yt = ytm[ql % 2]
                for h in range(8):
                    pv_ = pVg[h // 4]
                    if h % 2 == 0:
                        P.op('act', lambda e, h=h, pv_=pv_: e.activation(out=yt.t[:, h, :], in_=pv_.t[:, h % 4, 0:64],
                                                                         func=AF.Copy, scale=d8.t[:, h:h + 1]),
                             [pv_.b, d8.b], [yt.b])
                    else:
                        P.op('dve', lambda e, h=h, pv_=pv_: e.tensor_scalar(out=yt.t[:, h, :], in0=pv_.t[:, h % 4, 0:64],
                                                                            scalar1=d8.t[:, h:h + 1], scalar2=None,
                                                                            op0=ALU.mult), [pv_.b, d8.b], [yt.b])
                pst = pS3[1]
                pTv = pst.t[:, 0, :].bitcast(BF16)
                for hp in range(4):
                    P.op('pe', lambda e, hp=hp: e.transpose(out=pTv[:, hp * 128:(hp + 1) * 128],
                                                            in_=yt.t[:, 2 * hp:2 * hp + 2, :].rearrange("p h d -> p (h d)"),
                                                            identity=ident.t[:, :]), [yt.b, ident.b], [pst.b])
                P.op('dve', lambda e: e.tensor_copy(out=yo.t[:, :, ql * 128:(ql + 1) * 128],
                                                    in_=pTv[:, 0:512].rearrange("p (hp i) -> p hp i", hp=4)),
                     [pst.b], [yo.b])

            stage1(0)
            for ql in range(4):
                if ql + 1 < 4:
                    stage1(ql + 1)
                stage2(ql)
            P.dma('pool', ybdst.rearrange("hp q n -> q hp n")[:, :, qb + ti * 512:qb + (ti + 1) * 512], yo.t[:, :, :],
                  [yo.b], [DB['ybT']], yo.b)
    P.end_phase()


def phase2c(nc, P, cfg, I, S, DB):
    P.begin_phase()
    wpa = P.sb('wpa', [128, 4, D], BF16, dma=True)
    wpb = P.sb('wpb', [128, 4, D], BF16, dma=True)
    for i_, w_ in enumerate([wpa, wpb]):
        src = S['wpab'][i_].rearrange("p (hp h2 n) -> p hp h2 n", hp=4, h2=2)
        P.dmas('sp', [(w_.t[h2 * 64:(h2 + 1) * 64, :, :], src[:, :, h2, :]) for h2 in range(2)], [DB['wpab']], [w_.b],
               w_.b)
    wo = P.sb('wo', [128, 8, D], BF16, dma=True)
    P.dma('sp', wo.t[:, :, :], S['wout'].rearrange("f p n -> p f n"), [DB['wout']], [wo.b], wo.b)
    gp = P.sb('gpm', [128, D], F32, dma=True)
    P.dma('sp', gp.t[:, :], I['gpost'][:, D:2 * D], [], [gp.b], gp.b)
    mhalf = P.sb('mhalf2', [128, 8], F32)
    P.op('pool', lambda e: e.memset(mhalf.t[:, :], -0.5), [], [mhalf.b])
    ya = [P.sb('ya%d' % i, [128, 4, 512], BF16, dma=True) for i in range(2)]
    yb = [P.sb('yb%d' % i, [128, 4, 512], BF16, dma=True) for i in range(2)]
    ga = [P.sb('ga%d' % i, [128, 8, 512], BF16, dma=True) for i in range(2)]
    gb = [P.sb('gb%d' % i, [128, 8, 512], BF16, dma=True) for i in range(2)]
    hin = [P.sb('hin%d' % i, [128, 4, D], F32, dma=True) for i in range(2)]
    mT = P.sb('mT', [128, 8, 512], BF16)
    ta = [P.sb('ta%d' % i, [128, 512], F32) for i in range(2)]
    tbb = [P.sb('tb%d' % i, [128, 512], F32) for i in range(2)]
    ty = [P.sb('tym%d' % i, [128, D], F32) for i in range(2)]
    h2o = [P.sb('h2o%d' % i, [128, D], F32, dma='sw') for i in range(2)]
    junk = P.sb('junkm', [128, D], BF16)
    stat = [P.sb('statm%d' % i, [128, 8], F32) for i in range(4)]
    pP = [P.ps('pP%d' % i, [128, 512], F32) for i in range(4)]
    pM = [P.ps('pM%d' % i, [128, D], F32) for i in range(2)]
    nt = cfg.NQ // 512
    for ti in range(nt):
        a, b_, g1, g2, hi = ya[ti % 2], yb[ti % 2], ga[ti % 2], gb[ti % 2], hin[ti % 2]
        c0 = ti * 512
        for (t_, nm_) in [(a, 'yaT'), (b_, 'ybT')]:
            src = S[nm_].rearrange("(hp h2) p n -> p hp h2 n", h2=2)
            P.dmas('sp', [(t_.t[h2 * 64:(h2 + 1) * 64, :, :], src[:, :, h2, c0:c0 + 512]) for h2 in range(2)],
                   [DB[nm_]], [t_.b], t_.b)
        P.dma('sp', g1.t[:, :, :], S['sga'].rearrange("k p n -> p k n")[:, :, c0:c0 + 512], [DB['sga']], [g1.b], g1.b)
        P.dma('sp', g2.t[:, :, :], S['sgb'].rearrange("k p n -> p k n")[:, :, c0:c0 + 512], [DB['sgb']], [g2.b], g2.b)
        P.dma('sp', hi.t[:, :, :], S['hsp'][c0:c0 + 512, :].rearrange("(s p) d -> p s d", p=128), [DB['hsp']], [hi.b],
              hi.b)
        for k in range(8):
            pa = pP[(2 * k) % 4]
            pb = pP[(2 * k + 1) % 4]
            for (w, y, pp) in [(wpa, a, pa), (wpb, b_, pb)]:
                for h in range(4):
                    P.op('pe', lambda e, w=w, y=y, pp=pp, h=h, k=k: e.matmul(pp.t[:, :],
                                                                              lhsT=w.t[:, h, k * 128:(k + 1) * 128],
                                                                              rhs=y.t[:, h, :], start=(h == 0),
                                                                              stop=(h == 3)), [w.b, y.b], [pp.b])
            x1 = ta[k % 2]
            x2 = tbb[k % 2]
            P.op('dve', lambda e, pa=pa, x1=x1, k=k: e.tensor_tensor(out=x1.t[:, :], in0=pa.t[:, :], in1=g1.t[:, k, :],
                                                                     op=ALU.mult), [pa.b, g1.b], [x1.b])
            P.op('dve', lambda e, pb=pb, x2=x2, k=k: e.tensor_tensor(out=x2.t[:, :], in0=pb.t[:, :], in1=g2.t[:, k, :],
                                                                     op=ALU.mult), [pb.b, g2.b], [x2.b])
            P.op('pool', lambda e, x1=x1, x2=x2, k=k: e.tensor_tensor(out=mT.t[:, k, :], in0=x1.t[:, :], in1=x2.t[:, :],
                                                                      op=ALU.add), [x1.b, x2.b], [mT.b])
        for sub in range(4):
            pm = pM[sub % 2]
            for half in range(2):
                for k in range(8):
                    P.op('pe', lambda e, pm=pm, half=half, k=k, sub=sub: e.matmul(
                        pm.t[:, half * 512:(half + 1) * 512], lhsT=mT.t[:, k, sub * 128:(sub + 1) * 128],
                        rhs=wo.t[:, k, half * 512:(half + 1) * 512], start=(k == 0), stop=(k == 7)), [mT.b, wo.b],
                        [pm.b])
            st = stat[sub % 4]
            P.op('act', lambda e, pm=pm, st=st: e.activation(out=junk.t[:, :], in_=pm.t[:, :], func=AF.Square,
                                                             accum_out=st.t[:, 0:1]), [pm.b], [junk.b, st.b])
            rstd_from_ss(P, (st.t[:, 0:1], st.b), (st.t[:, 2:3], st.b), D, (mhalf.t[:, 0:1], mhalf.b),
                         (st.t[:, 1:2], st.b))
            t_ = ty[sub % 2]
            P.op('dve', lambda e, pm=pm, st=st, t_=t_: e.scalar_tensor_tensor(out=t_.t[:, :], in0=pm.t[:, :],
                                                                              scalar=st.t[:, 2:3], in1=gp.t[:, :],
                                                                              op0=ALU.mult, op1=ALU.mult),
                 [pm.b, st.b, gp.b], [t_.b])
            ho = h2o[sub % 2]
            P.op('pool', lambda e, t_=t_, ho=ho, sub=sub: e.tensor_tensor(out=ho.t[:, :], in0=t_.t[:, :],
                                                                         in1=hi.t[:, sub, :], op=ALU.add),
                 [t_.b, hi.b], [ho.b])
            r0 = c0 + sub * 128
            P.dma('pool', S['h2sp'][r0:r0 + 128, :], ho.t[:, :], [ho.b], [DB['h2sp']], ho.b)
    P.end_phase()


def phase3(nc, P, cfg, I, S, DB, y_out):
    P.begin_phase()
    nt = cfg.NQ // 512
    plan = []
    for ti in range(nt):
        plan += ffn_plan(S['wffn2'], EARLY if ti > 0 else 0, NF)
        if ti + 1 < nt:
            plan += ffn_plan(S['wffn2'], 0, EARLY)
    ring = Ring(P, plan, n=8)
    fc = FfnCtx(nc, P, I['gpost'][:, 2 * D:3 * D], I['ident'], S['wffn2'], nhout=4)
    gpe = P.sb('gpe', [128, D], F32, dma=True)
    P.dma('sp', gpe.t[:, :], I['gpost'][:, 3 * D:4 * D], [], [gpe.b], gpe.b)
    wpg = P.sb('wpg', [128, 8, D], BF16, dma=True)
    P.dma('sp', wpg.t[:, :, :], S['wpg'].rearrange("f p n -> p f n"), [DB['wpg']], [wpg.b], wpg.b)
    wpe = P.sb('wpe', [128, 2, D], BF16, dma=True)
    P.dma('sp', wpe.t[:, :, :], S['wpe'].rearrange("f p n -> p f n"), [DB['wpe']], [wpe.b], wpe.b)
    hT = P.sb('hT3', [128, 8, 512], BF16)
    pin = P.sb('pin', [128, 4, PLE], F32, dma=True)
    pbf = [P.sb('pbf%d' % i, [128, PLE], BF16) for i in range(2)]
    pT_sb = [P.sb('pTs%d' % i, [128, 2, 128], BF16) for i in range(2)]
    sg = P.sb('sg', [128, D], BF16)
    ev = [P.sb('ev%d' % i, [128, D], F32, dma='sw') for i in range(2)]
    fc.first(S['h2sp'][0:512, :])
    for ti in range(nt):
        c0 = ti * 512
        pi = pin
        P.dma('sp', pi.t[:, :, :], I['p_own'][c0:c0 + 512, :].rearrange("(s p) d -> p s d", p=128), [], [pi.b], pi.b)
        hos = {}
        hxb = {}

        def cb_pre(sub, ho):
            hos[sub] = ho
            hxb[sub] = fc.xnb[sub % 2]
            fc.norm_pre(ho.t[:, :], ho.b, hxb[sub])
            pb_ = pbf[sub % 2]
            P.op('dve', lambda e: e.tensor_copy(out=pb_.t[:, :], in_=pi.t[:, sub, :]), [pi.b], [pb_.b])

        def cb_T(sub):
            fc.norm_tr(hxb[sub], hT, sub, fc.pT[0])
            pb_ = pbf[sub % 2]
            ptp = fc.pT[0]
            for c in range(2):
                P.op('pe', lambda e, c=c: e.transpose(out=ptp.t[:, c * 128:(c + 1) * 128],
                                                      in_=pb_.t[:, c * 128:(c + 1) * 128], identity=fc.ident.t[:, :]),
                     [pb_.b, fc.ident.b], [ptp.b])
            pts = pT_sb[sub % 2]
            P.op('dve', lambda e: e.tensor_copy(out=pts.t[:, :, :],
                                                in_=ptp.t[:, 0:256].rearrange("p (c n) -> p c n", c=2)),
                 [ptp.b], [pts.b])

        def cb_mm(sub, c0=c0):
            ho = hos[sub]
            pts = pT_sb[sub % 2]
            s_ = sg
            e_ = ev[sub % 2]
            for half in range(2):
                pg = fc.nextA()
                pe_ = fc.nextA()
                for c in range(8):
                    P.op('pe', lambda e, c=c: e.matmul(pg.t[:, :], lhsT=hT.t[:, c, sub * 128:(sub + 1) * 128],
                                                       rhs=wpg.t[:, c, half * 512:(half + 1) * 512], start=(c == 0),
                                                       stop=(c == 7)), [hT.b, wpg.b], [pg.b])
                for c in range(2):
                    P.op('pe', lambda e, c=c: e.matmul(pe_.t[:, :], lhsT=pts.t[:, c, :],
                                                       rhs=wpe.t[:, c, half * 512:(half + 1) * 512], start=(c == 0),
                                                       stop=(c == 1)), [pts.b, wpe.b], [pe_.b])
                P.op('act', lambda e: e.activation(out=s_.t[:, half * 512:(half + 1) * 512], in_=pg.t[:, :],
                                                   func=AF.Sigmoid), [pg.b], [s_.b])
                P.op('dve', lambda e: e.tensor_tensor(out=e_.t[:, half * 512:(half + 1) * 512], in0=pe_.t[:, :],
                                                      in1=s_.t[:, half * 512:(half + 1) * 512], op=ALU.mult),
                     [pe_.b, s_.b], [e_.b])
            st = fc.nstat()
            P.op('act', lambda e: e.activation(out=fc.junk.t[:, :], in_=e_.t[:, :], func=AF.Square,
                                               accum_out=st.t[:, 0:1]), [e_.b], [fc.junk.b, st.b])
            rstd_from_ss(P, (st.t[:, 0:1], st.b), (st.t[:, 2:3], st.b), D, (fc.mhalf.t[:, 0:1], fc.mhalf.b),
                         (st.t[:, 1:2], st.b))
            P.op('dve', lambda e: e.scalar_tensor_tensor(out=e_.t[:, :], in0=e_.t[:, :], scalar=st.t[:, 2:3],
                                                         in1=gpe.t[:, :], op0=ALU.mult, op1=ALU.mult),
                 [e_.b, st.b, gpe.b], [e_.b])
            P.op('pool', lambda e: e.tensor_tensor(out=e_.t[:, :], in0=e_.t[:, :], in1=ho.t[:, :], op=ALU.add),
                 [e_.b, ho.b], [e_.b])
            r0 = c0 + sub * 128
            P.dma('pool', y_out[r0:r0 + 128, :], e_.t[:, :], [e_.b], [DB['y_out']], e_.b)
        nsrc = S['h2sp'][c0 + 512:c0 + 1024, :] if ti + 1 < nt else None
        fc.run_tile(ring, nsrc, cb_pre, cb_T, cb_mm)
    P.end_phase()


def rope_tables(pos):
    inv = (1.0 / (np.float32(10000.0) ** (np.arange(0, 32, 2, dtype=np.float32) / np.float32(32)))).astype(np.float32)
    ang = (pos.astype(np.float32)[:, None] * inv[None, :]).astype(np.float32)
    c = np.cos(ang.astype(np.float64)).astype(np.float32).T
    s = np.sin(ang.astype(np.float64)).astype(np.float32).T
    n = pos.shape[0]
    C = np.ones((96, n), np.float32)
    Sn = np.zeros((96, n), np.float32)
    C[64:80] = c
    C[80:96] = c
    Sn[64:80] = -s
    Sn[80:96] = s
    return C, Sn


def t5_bucket_np(rel):
    nb = 16
    max_exact = 8
    ret = np.where(rel > 0, nb, 0)
    n = np.abs(rel)
    nf = np.maximum(n, 1).astype(np.float32)
    large = max_exact + (np.log(nf / np.float32(max_exact)) / np.float32(np.log(128 / max_exact))
                         * np.float32(nb - max_exact)).astype(np.int32)
    large = np.minimum(large, nb - 1)
    return ret + np.where(n < max_exact, n, large)


def onehot_table():
    j = np.arange(128)[:, None]
    i = np.arange(128)[None, :]
    oh = np.zeros((33, 3, 128, 128), np.float32)
    for kb in range(3):
        rel = (kb - 1) * 128 + j - i
        valid = np.abs(rel) <= 128
        bk = t5_bucket_np(rel)
        for b in range(32):
            oh[b, kb] = ((bk == b) & valid).astype(np.float32)
        oh[32, kb] = (~valid).astype(np.float32)
    return oh.reshape(33, -1)


def shared_inputs(inp):
    g = lambda k: np.asarray(inp[k], np.float32)[0]
    sh = {}

    def cols(v):
        return np.ascontiguousarray(v.reshape(-1, 128).T)
    sh['gcols'] = np.concatenate([cols(g('ffn1_pre_g')), cols(g('mix_pre_g')), cols(g('q_norm_g')),
                                  cols(g('kv_norm_g')), cols(g('ffn2_pre_g')), cols(g('ple_pre_g'))], axis=1)
    gp = np.concatenate([g('ffn1_post_g'), g('mix_post_g'), g('ffn2_post_g'), g('ple_post_g')])
    sh['gpost'] = np.ascontiguousarray(np.broadcast_to(gp[None, :], (128, 4 * D)))
    sh['rb_ext'] = np.concatenate([np.asarray(inp['rel_bias'], np.float32), np.full((1, 8), NEG, np.float32)], axis=0)
    sh['onehot'] = onehot_table()
    sh['sinkb'] = np.ascontiguousarray(np.broadcast_to(g('sink')[None, :], (128, 8)))
    sh['ident'] = np.eye(128, dtype=np.float32).astype(ml_dtypes.bfloat16)
    shf = np.zeros((128, 64), np.float32)
    shf[64 + np.arange(64), np.arange(64)] = 1.0
    sh['shift'] = shf
    for n in ['ffn1', 'ffn2']:
        for w in ['w1', 'w3', 'w2']:
            sh[n + '_' + w] = g(n + '_' + w)
    w_in = g('w_in')
    o = np.cumsum([0, 384, 256, 32, 512, 128, 128, 1024, 1024])
    cq, ckv, kr, qb, kb, vb, ga, gb = [w_in[:, o[i]:o[i + 1]] for i in range(8)]
    z64 = np.zeros((D, 64), np.float32)
    z32 = np.zeros((D, 32), np.float32)
    krs = np.concatenate([kr[:, 16:32], kr[:, 0:16]], axis=1)
    sh['w_in'] = np.ascontiguousarray(np.concatenate(
        [ckv, z64, kr, z32, z64, krs, z32, kb, vb, cq, qb, ga, gb], axis=1))
    assert sh['w_in'].shape[1] == 29 * 128
    wq = g('w_uq')
    slots = []
    zq = np.zeros((384, 32), np.float32)
    for h in range(8):
        blk = wq[:, h * 96:(h + 1) * 96]
        slots.append(np.concatenate([blk, zq], axis=1))
        sw = np.concatenate([blk[:, 0:64], blk[:, 80:96], blk[:, 64:80]], axis=1)
        slots.append(np.concatenate([sw, zq], axis=1))
    sh['w_uq'] = np.ascontiguousarray(np.concatenate(slots, axis=1))
    sh['w_uk'] = g('w_uk')
    sh['w_uv'] = g('w_uv')
    sh['w_pa'] = g('w_proj_a')
    sh['w_pb'] = g('w_proj_b')
    sh['w_out'] = g('w_out')
    sh['w_pg'] = g('w_ple_gate')
    sh['w_pe'] = g('w_ple_proj')
    return sh


def core_inputs(cfg, prompts_x, prompts_p, samp_x, samp_p, r, nquart):
    SS, QS = cfg.SS, cfg.QS
    start = r * QS - 512
    idx = (start + np.arange(SS)) % SS
    xs = [np.asarray(a, np.float32) for a in prompts_x] + [np.asarray(samp_x, np.float32)[idx]]
    ps_ = [np.asarray(a, np.float32) for a in prompts_p] + [np.asarray(samp_p, np.float32)[r * QS:(r + 1) * QS]]
    Cs, Ss = [], []
    for a in prompts_x:
        C, Sn = rope_tables(np.arange(a.shape[0]))
        Cs.append(C)
        Ss.append(Sn)
    C, Sn = rope_tables(idx)
    Cs.append(C)
    Ss.append(Sn)
    edge = np.zeros((128, 2), np.float32)
    if r == 0:
        edge[:, 0] = NEG
    if r == nquart - 1:
        edge[:, 1] = NEG
    return dict(x_all=np.ascontiguousarray(np.concatenate(xs, axis=0)),
                p_own=np.ascontiguousarray(np.concatenate(ps_, axis=0)),
                ropeC=np.ascontiguousarray(np.concatenate(Cs, axis=1)),
                ropeS=np.ascontiguousarray(np.concatenate(Ss, axis=1)), edge=edge)


_NC_CACHE = {}


def kernel(**inp):
    cfg = Cfg()
    if 'nc' not in _NC_CACHE:
        _NC_CACHE['nc'] = build(cfg)
    nc = _NC_CACHE['nc']
    sh = shared_inputs(inp)
    xp = np.asarray(inp['x_prompt'], np.float32)
    xs = np.asarray(inp['x_sample'], np.float32)
    pp = np.asarray(inp['p_prompt'], np.float32)[0]
    psm = np.asarray(inp['p_sample'], np.float32)[0]
    in_maps = []
    for c in range(8):
        b, r = c // 4, c % 4
        ci = core_inputs(cfg, [xp[2 * c], xp[2 * c + 1]], [pp[2 * c], pp[2 * c + 1]], xs[b], psm[b], r, 4)
        m = dict(sh)
        m.update(ci)
        in_maps.append(m)
    res = run_bass_kernel_spmd(nc, in_maps, core_ids=list(range(8)))
    y_p = np.zeros((16, 2048, D), np.float32)
    y_s = np.zeros((2, 16384, D), np.float32)
    for c in range(8):
        y = np.asarray(res.results[c]['y_own'], np.float32)
        y_p[2 * c] = y[0:2048]
        y_p[2 * c + 1] = y[2048:4096]
        b, r = c // 4, c % 4
        y_s[b, r * 4096:(r + 1) * 4096] = y[4096:8192]
    return (y_p, y_s)
```

```python
import types
import numpy as np
import ml_dtypes
from contextlib import ExitStack
import concourse.bass as bass
import concourse.mybir as mybir
from concourse.bass_utils import run_bass_kernel_spmd

F32 = mybir.dt.float32
BF16 = mybir.dt.bfloat16
AF = mybir.ActivationFunctionType
ALU = mybir.AluOpType

D = 1024
DFF = 2816
NF = 22
PLE = 256
EPS = 1e-6
NEG = -1e30
NRING = 48
ENG = ['sp', 'pe', 'act', 'dve', 'pool']


class Cfg:
    def __init__(s, NP=2, SP=2048, SS=16384, QS=4096):
        s.NP, s.SP, s.SS, s.QS = NP, SP, SS, QS
        s.jobs = []
        tb = 0
        qb = 0
        for i in range(NP):
            s.jobs.append(dict(name='p%d' % i, ntok=SP, q0=0, nq=SP, sample=False, tb=tb, qb=qb))
            tb += SP
            qb += SP
        s.jobs.append(dict(name='s', ntok=SS, q0=512, nq=QS, sample=True, tb=tb, qb=qb))
        tb += SS
        qb += QS
        s.NTOK = tb
        s.NQ = qb


class Buf:
    def __init__(s, name, dsem=None, multi=False, psum=False):
        s.name = name
        s.psum = psum
        s.wev = {}
        s.rev = {}
        s.dsem = dsem
        s.multi = multi


class Tile:
    def __init__(s, t, b):
        s.t = t
        s.b = b

    def __getitem__(s, k):
        return s.t[k]


def _freeze(fn):
    if fn is None or fn.__closure__ is None:
        return fn
    cells = []
    for c in fn.__closure__:
        try:
            cells.append(types.CellType(c.cell_contents))
        except ValueError:
            cells.append(c)
    return types.FunctionType(fn.__code__, fn.__globals__, fn.__name__, fn.__defaults__, tuple(cells))


class Prog:
    def __init__(s, nc, gstack):
        s.nc = nc
        s.gstack = gstack
        s.semh = {}
        s.cnt = {}
        s.q = {e: [] for e in ENG}
        s.waited = {e: {} for e in ENG}
        s.esem = {e: s.new_sem('es_' + e) for e in ENG}
        s.dpool = {'hw': [s.new_sem('dh%d' % i) for i in range(48)], 'sw': [s.new_sem('dw%d' % i) for i in range(24)]}
        s.dnext = {'hw': 0, 'sw': 0}
        s.stack = None
        s.nins = 0

    def new_sem(s, name):
        h = s.gstack.enter_context(s.nc.semaphore(name))
        k = len(s.semh)
        s.semh[k] = h
        s.cnt[k] = 0
        return k

    def begin_phase(s):
        s.stack = ExitStack()
        s.phase = getattr(s, 'phase', -1) + 1
        s.dnext = {'hw': 0, 'sw': 0}
        s.q = {e: [] for e in ENG}

    def take_dsem(s, kind):
        k = s.dpool[kind][s.dnext[kind]]
        s.dnext[kind] += 1
        return k

    def sb(s, name, shape, dt, dma=False):
        name = 'f%d_%s' % (s.phase, name)
        t = s.stack.enter_context(s.nc.sbuf_tensor(name, list(shape), dt))
        return Tile(t, Buf(name, dsem=s.take_dsem('hw' if dma is True else dma) if dma else None))

    def ps(s, name, shape, dt):
        name = 'f%d_%s' % (s.phase, name)
        t = s.stack.enter_context(s.nc.psum_tensor(name, list(shape), dt))
        return Tile(t, Buf(name, psum=True))

    def op(s, eng, fn, reads=(), writes=(), dsem=None, ninc=1):
        need = {}

        def add(d):
            for k, v in d.items():
                if v > need.get(k, 0):
                    need[k] = v
        for b in reads:
            add(b.wev)
            if b.psum:
                add(b.rev)
        for b in writes:
            if not b.multi:
                add(b.wev)
            add(b.rev)
        own = s.esem[eng]
        waits = []
        for k, v in need.items():
            if k == own and eng == 'pe' and dsem is None:
                continue
            if s.waited[eng].get(k, 0) >= v:
                continue
            s.waited[eng][k] = v
            waits.append((k, v))
        if dsem is None:
            k = own
            s.cnt[k] += 1
            inc = 1
        else:
            k = dsem
            s.cnt[k] += 16 * ninc
            inc = 16
        v = s.cnt[k]
        s.q[eng].append((waits, _freeze(fn), k, inc))
        s.nins += 1
        for b in reads:
            if v > b.rev.get(k, 0):
                b.rev[k] = v
        for b in writes:
            if b.multi:
                if v > b.wev.get(k, 0):
                    b.wev[k] = v
            else:
                b.wev = {k: v}
                b.rev = {}

    def dma(s, eng, out, in_, reads, writes, sbuf_side):
        s.op(eng, lambda e: e.dma_start(out=out, in_=in_), reads=reads, writes=writes, dsem=sbuf_side.dsem)

    def dmas(s, eng, pairs, reads, writes, sbuf_side):
        s.op(eng, lambda e: [e.dma_start(out=o, in_=i) for (o, i) in pairs], reads=reads, writes=writes,
             dsem=sbuf_side.dsem, ninc=len(pairs))

    def end_phase(s):
        for e in ENG:
            waits = []
            for k, v in s.cnt.items():
                if v > 0 and s.waited[e].get(k, 0) < v:
                    s.waited[e][k] = v
                    waits.append((k, v))
            s.q[e].append((waits, None, None, None))
        with s.nc.Block() as blk:
            for eng, dec in [('sp', blk.sync), ('pe', blk.tensor), ('act', blk.scalar), ('dve', blk.vector),
                             ('pool', blk.gpsimd)]:
                items = s.q[eng]

                def body(e, items=items):
                    for waits, fn, k, inc in items:
                        for (wk, wv) in waits:
                            e.wait_ge(s.semh[wk], wv)
                        if fn is None:
                            continue
                        r = fn(e)
                        if not isinstance(r, (list, tuple)):
                            r = [r]
                        for ins in r:
                            ins.then_inc(s.semh[k], inc)
                dec(body)
        s.stack.close()
        s.stack = None


class Ring:
    def __init__(s, P, plan, n=NRING):
        s.P = P
        s.plan = plan
        s.n = min(n, max(1, len(plan)))
        s.slots = [P.sb('ring%d' % i, [128, 1024], BF16, dma=True) for i in range(s.n)]
        s.loaded = 0
        s.used = 0
        s.released = 0
        for _ in range(s.n):
            s._load()

    def _load(s):
        if s.loaded >= len(s.plan):
            return
        key, src, ncol = s.plan[s.loaded]
        sl = s.slots[s.loaded % s.n]
        s.P.dma('sp', sl.t[:, 0:ncol], src, reads=[], writes=[sl.b], sbuf_side=sl.b)
        s.loaded += 1

    def next(s, key):
        k, src, ncol = s.plan[s.used]
        assert k == key, (k, key)
        assert s.used < s.loaded
        sl = s.slots[s.used % s.n]
        s.used += 1
        return sl

    def release(s, cnt=1):
        for _ in range(cnt):
            s.released += 1
            assert s.released <= s.used
            s._load()


def build(cfg, debug=False):
    nc = bass.Bass("TRN2", target_bir_lowering=False)
    NTOK, NQ = cfg.NTOK, cfg.NQ
    okind = "ExternalOutput" if debug else "Internal"

    def din(name, shape, dt=F32):
        return nc.dram_tensor(name, list(shape), dt, kind="ExternalInput").ap()

    def dscr(name, shape, dt):
        return nc.dram_tensor(name, list(shape), dt, kind=okind).ap()

    I = {}
    I['x_all'] = din('x_all', [NTOK, D])
    I['p_own'] = din('p_own', [NQ, PLE])
    I['ropeC'] = din('ropeC', [96, NTOK])
    I['ropeS'] = din('ropeS', [96, NTOK])
    I['edge'] = din('edge', [128, 2])
    I['gcols'] = din('gcols', [128, 37])
    I['gpost'] = din('gpost', [128, 4 * D])
    I['rb_ext'] = din('rb_ext', [33, 8])
    I['onehot'] = din('onehot', [33, 3 * 128 * 128])
    I['sinkb'] = din('sinkb', [128, 8])
    I['ident'] = din('ident', [128, 128], BF16)
    I['shift'] = din('shift', [128, 64])
    for n in ['ffn1', 'ffn2']:
        I[n + '_w1'] = din(n + '_w1', [D, DFF])
        I[n + '_w3'] = din(n + '_w3', [D, DFF])
        I[n + '_w2'] = din(n + '_w2', [DFF, D])
    I['w_in'] = din('w_in', [D, 29 * 128])
    I['w_uq'] = din('w_uq', [384, 16 * 128])
    I['w_uk'] = din('w_uk', [256, 512])
    I['w_uv'] = din('w_uv', [256, 512])
    I['w_pa'] = din('w_pa', [512, D])
    I['w_pb'] = din('w_pb', [512, D])
    I['w_out'] = din('w_out', [D, D])
    I['w_pg'] = din('w_pg', [D, D])
    I['w_pe'] = din('w_pe', [PLE, D])
    y_out = nc.dram_tensor('y_own', [NQ, D], F32, kind="ExternalOutput").ap()

    S = {}
    S['wffn1'] = dscr('wffn1', [66, 128, 1024], BF16)
    S['wffn2'] = dscr('wffn2', [66, 128, 1024], BF16)
    S['win'] = dscr('win', [29, 128, 1024], BF16)
    S['wuq'] = dscr('wuq', [16, 128, 384], BF16)
    S['wukv'] = dscr('wukv', [2, 128, 1024], BF16)
    S['wpab'] = dscr('wpab', [2, 64, 8 * 1024], BF16)
    S['wout'] = dscr('wout', [8, 128, 1024], BF16)
    S['wpg'] = dscr('wpg', [8, 128, 1024], BF16)
    S['wpe'] = dscr('wpe', [2, 128, 1024], BF16)
    S['biasD'] = dscr('biasD', [8, 3 * 128 * 128], F32)
    S['hsp'] = dscr('hsp', [NQ, D], F32)
    S['h2sp'] = dscr('h2sp', [NQ, D], F32)
    S['ckvnT'] = dscr('ckvnT', [256, NTOK], BF16)
    S['kropeT'] = dscr('kropeT', [32, NTOK], BF16)
    S['kbT'] = dscr('kbT', [2, 64, NTOK], BF16)
    S['vbs'] = dscr('vbs', [NTOK, 128], BF16)
    S['QT'] = dscr('QT', [8, 96, NQ], BF16)
    S['qbT'] = dscr('qbT', [8, 64, NQ], BF16)
    S['sga'] = dscr('sga', [8, 128, NQ], BF16)
    S['sgb'] = dscr('sgb', [8, 128, NQ], BF16)
    S['yaT'] = dscr('yaT', [8, 64, NQ], BF16)
    S['ybT'] = dscr('ybT', [8, 64, NQ], BF16)
    DB = {k: Buf('d_' + k, multi=True) for k in list(S.keys()) + ['y_out']}

    gstack = ExitStack()
    P = Prog(nc, gstack)

    phase0(nc, P, cfg, I, S, DB)
    if getattr(cfg, 'stop_after', 9) >= 1:
        phase1(nc, P, cfg, I, S, DB)
    if getattr(cfg, 'stop_after', 9) >= 2:
        phase2a(nc, P, cfg, I, S, DB)
        phase2b(nc, P, cfg, I, S, DB)
    if getattr(cfg, 'stop_after', 9) >= 3:
        phase2c(nc, P, cfg, I, S, DB)
        phase3(nc, P, cfg, I, S, DB, y_out)
    gstack.close()
    return nc


def phase0(nc, P, cfg, I, S, DB):
    P.begin_phase()
    gc = P.sb('gc', [128, 37], F32, dma=True)
    P.dma('sp', gc[:, :], I['gcols'], [], [gc.b], gc.b)
    NB = 4
    stg = [P.sb('stg%d' % i, [128, 8, 512], F32, dma=True) for i in range(NB)]
    cvt = [P.sb('cvt%d' % i, [128, 8, 512], BF16, dma='sw') for i in range(NB)]
    state = dict(i=0)
    engs = ['dve', 'act']

    def conv(src, C, W, dsts, gcol0, st_view=None):
        i = state['i']
        state['i'] += 1
        st = stg[i % NB]
        cv = cvt[i % NB]
        eng = engs[i % 2]
        P.dma('sp', st.t[:, 0:C, 0:W] if st_view is None else st_view(st.t), src, [], [st.b], st.b)
        if gcol0 is None:
            if eng == 'act':
                P.op('act', lambda e: e.activation(out=cv.t[:, 0:C, 0:W], in_=st.t[:, 0:C, 0:W], func=AF.Copy),
                     [st.b], [cv.b])
            else:
                P.op(eng, lambda e: e.tensor_copy(out=cv.t[:, 0:C, 0:W], in_=st.t[:, 0:C, 0:W]), [st.b], [cv.b])
        else:
            for c in range(C):
                sc = gc.t[:, gcol0 + c:gcol0 + c + 1]
                if eng == 'act':
                    P.op('act', lambda e, c=c, sc=sc: e.activation(out=cv.t[:, c, 0:W], in_=st.t[:, c, 0:W],
                                                                     func=AF.Copy, scale=sc), [st.b, gc.b], [cv.b])
                else:
                    P.op(eng, lambda e, c=c, sc=sc: e.tensor_scalar(out=cv.t[:, c, 0:W], in0=st.t[:, c, 0:W],
                                                                      scalar1=sc, scalar2=None, op0=ALU.mult),
                         [st.b, gc.b], [cv.b])
        pairs = [(d, f(cv.t)) for (d, f) in dsts]
        P.dmas('pool', pairs, [cv.b], [], cv.b)

    def conv_U(w, K, ncols, dst, slot_of, gcol0, Wp=512):
        C = K // 128
        wv = w.rearrange("(c p) n -> p c n", p=128)
        for j0 in range(0, ncols, Wp):
            Wc = min(Wp, ncols - j0)
            dsts = []
            for jj in range(Wc // 128):
                sl = slot_of((j0 // 128) + jj)
                d = dst[sl][:, 0:C * 128].rearrange("p (c n) -> p c n", c=C)
                dsts.append((d, lambda t, jj=jj, C=C: t[:, 0:C, jj * 128:(jj + 1) * 128]))
            conv(wv[:, :, j0:j0 + Wc], C, Wc, dsts, gcol0)

    def conv_D(w, K, dst, slot0, gcol0):
        Fn = K // 128
        wv = w.rearrange("(f p) n -> p f n", p=128)
        for f0 in range(0, Fn, 4):
            fc = min(4, Fn - f0)
            src = wv[:, f0:f0 + fc, :].rearrange("p f (h n) -> p f h n", h=2)
            dsts = []
            for ff in range(fc):
                d = dst[slot0 + f0 + ff].rearrange("p (h n) -> p h n", h=2)
                dsts.append((d, lambda t, ff=ff: t[:, 2 * ff:2 * ff + 2, :]))
            conv(src, fc * 2, 512, dsts, None,
                 st_view=lambda t, fc=fc: t[:, 0:2 * fc, :].rearrange("p (f h) n -> p f h n", h=2))

    G_F1, G_MIX, G_Q, G_KV, G_F2, G_PLE = 0, 8, 16, 19, 21, 29
    for n, g0 in [('ffn1', G_F1), ('ffn2', G_F2)]:
        dst = S['w' + n]
        conv_U(I[n + '_w1'], D, DFF, dst, lambda j: 2 * j, g0)
        conv_U(I[n + '_w3'], D, DFF, dst, lambda j: 2 * j + 1, g0)
        conv_D(I[n + '_w2'], DFF, dst, 44, None)
    conv_U(I['w_in'], D, 29 * 128, S['win'], lambda j: j, G_MIX)
    conv_U(I['w_uq'], 384, 16 * 128, S['wuq'], lambda j: j, G_Q)
    for i, nm in enumerate(['w_uk', 'w_uv']):
        wv = I[nm].rearrange("(c p) n -> p c n", p=128)
        d = S['wukv'][i].rearrange("p (c n) -> p c n", c=2)
        conv(wv, 2, 512, [(d, lambda t: t[:, 0:2, :])], G_KV)
    for i, nm in enumerate(['w_pa', 'w_pb']):
        wv = I[nm].rearrange("(h p) n -> p h n", p=64)
        dd = S['wpab'][i].rearrange("p (h n) -> p h n", h=8)
        for hh in range(0, 8, 4):
            src = wv[:, hh:hh + 4, :].rearrange("p h (a n) -> p h a n", a=2)
            i0 = state['i']
            state['i'] += 1
            st = stg[i0 % NB]
            cv = cvt[i0 % NB]
            P.dma('sp', st.t[0:64, :, :].rearrange("p (h a) n -> p h a n", a=2), src, [], [st.b], st.b)
            P.op('dve', lambda e, st=st, cv=cv: e.tensor_copy(out=cv.t[0:64, :, :], in_=st.t[0:64, :, :]), [st.b],
                 [cv.b])
            d = dd[:, hh:hh + 4, :].rearrange("p h (a n) -> p h a n", a=2)
            P.dmas('pool', [(d, cv.t[0:64, :, :].rearrange("p (h a) n -> p h a n", a=2))], [cv.b], [], cv.b)
    conv_D(I['w_out'], D, S['wout'], 0, None)
    wv = I['w_pg'].rearrange("(f p) n -> p f n", p=128)
    for f0 in range(0, 8, 4):
        i0 = state['i']
        state['i'] += 1
        st = stg[i0 % NB]
        cv = cvt[i0 % NB]
        src = wv[:, f0:f0 + 4, :].rearrange("p f (h n) -> p f h n", h=2)
        P.dma('sp', st.t[:, :, :].rearrange("p (f h) n -> p f h n", h=2), src, [], [st.b], st.b)
        for ff in range(4):
            sc = gc.t[:, G_PLE + f0 + ff:G_PLE + f0 + ff + 1]
            P.op('dve', lambda e, ff=ff, sc=sc, st=st, cv=cv: e.tensor_scalar(
                out=cv.t[:, 2 * ff:2 * ff + 2, :], in0=st.t[:, 2 * ff:2 * ff + 2, :], scalar1=sc, scalar2=None,
                op0=ALU.mult), [st.b, gc.b], [cv.b])
        pairs = []
        for ff in range(4):
            d = S['wpg'][f0 + ff].rearrange("p (h n) -> p h n", h=2)
            pairs.append((d, cv.t[:, 2 * ff:2 * ff + 2, :]))
        P.dmas('pool', pairs, [cv.b], [], cv.b)
    conv_D(I['w_pe'], PLE, S['wpe'], 0, None)

    rb = P.sb('rb', [33, 8], F32, dma=True)
    P.dma('sp', rb[:, :], I['rb_ext'], [], [rb.b], rb.b)
    pb = [P.ps('pb%d' % i, [128, 512], F32) for i in range(4)]
    NCH = 3 * 128 * 128 // 2048
    oh = [P.sb('oh%d' % i, [33, 2048], F32, dma=True) for i in range(2)]
    bo = [P.sb('bo%d' % i, [8, 2048], F32, dma='sw') for i in range(2)]
    for ch in range(NCH):
        o = oh[ch % 2]
        b_ = bo[ch % 2]
        P.dma('sp', o[:, :], I['onehot'][:, ch * 2048:(ch + 1) * 2048], [], [o.b], o.b)
        for q in range(4):
            pp = pb[q]
            P.op('pe', lambda e, pp=pp, o=o, q=q: e.matmul(pp.t[0:8, :], lhsT=rb.t[:, :], rhs=o.t[:, q * 512:(q + 1) * 512],
                                                            start=True, stop=True), [rb.b, o.b], [pp.b])
            P.op('dve' if q % 2 == 0 else 'act',
                 (lambda e, pp=pp, b_=b_, q=q: e.tensor_copy(out=b_.t[:, q * 512:(q + 1) * 512], in_=pp.t[0:8, :]))
                 if q % 2 == 0 else
                 (lambda e, pp=pp, b_=b_, q=q: e.activation(out=b_.t[:, q * 512:(q + 1) * 512], in_=pp.t[0:8, :],
                                                             func=AF.Copy)),
                 [pp.b], [b_.b])
        P.dmas('pool', [(S['biasD'][:, ch * 2048:(ch + 1) * 2048], b_.t[:, :])], [b_.b], [DB['biasD']], b_.b)
    P.end_phase()


def rstd_from_ss(P, ss, out, n, half, tmp):
    P.op('dve', lambda e: e.tensor_scalar(out=tmp[0], in0=ss[0], scalar1=1.0 / n, scalar2=EPS, op0=ALU.mult,
                                          op1=ALU.add), [ss[1]], [tmp[1]])
    P.op('pool', lambda e: e.tensor_tensor(out=out[0], in0=tmp[0], in1=half[0], op=ALU.pow), [tmp[1], half[1]],
         [out[1]])


def rstd_big(P, ss, out, n, tmp):
    P.op('act', lambda e: e.activation(out=tmp.t[:, :], in_=ss.t[:, :], func=AF.Sqrt, scale=1.0 / n, bias=EPS),
         [ss.b], [tmp.b])
    P.op('dve', lambda e: e.reciprocal(out=out.t[:, :], in_=tmp.t[:, :]), [tmp.b], [out.b])


EARLY = 3


def ffn_plan(wd, f0=0, f1=NF):
    plan = []
    for f in range(f0, f1):
        plan.append((('u', f, 0), wd[2 * f], 1024))
        plan.append((('u', f, 1), wd[2 * f + 1], 1024))
    return plan


class FfnCtx:
    def __init__(s, nc, P, gpost_ap, ident_ap, wd, nhout=2):
        s.nc, s.P = nc, P
        s.w2 = P.sb('w2res', [128, NF, D], BF16, dma=True)
        w2src = wd[44:66].rearrange("f p n -> p f n")
        P.dmas('sp', [(s.w2.t[:, f0:min(NF, f0 + 6), :], w2src[:, f0:min(NF, f0 + 6), :]) for f0 in range(0, NF, 6)],
               [], [s.w2.b], s.w2.b)
        s.xin = [P.sb('xin%d' % i, [128, 4, D], F32, dma=True) for i in range(2)]
        s.xnb = [P.sb('xnb%d' % i, [128, D], BF16) for i in range(2)]
        s.xnbx = [P.sb('xnbx%d' % i, [128, D], BF16) for i in range(2)]
        s.xnT = P.sb('xnT', [128, 8, 512], BF16)
        s.actT = P.sb('actT', [128, NF, 512], BF16, dma='sw')
        s.sil = [P.sb('sil%d' % i, [128, 512], BF16) for i in range(2)]
        s.hout = [P.sb('hout%d' % i, [128, D], F32, dma='sw') for i in range(nhout)]
        s.nhout = nhout
        s.junk = P.sb('junk', [128, D], BF16)
        s.junk2 = s.junk
        s.stat = [P.sb('stat%d' % i, [128, 8], F32) for i in range(8)]
        s.mhalf = P.sb('mhalf', [128, 512], F32)
        s.gp = P.sb('gp', [128, D], F32, dma=True)
        s.ident = P.sb('identb', [128, 128], BF16, dma=True)
        pall = P.ps('pAll', [128, 4, 512], F32)
        s.pA = [Tile(pall.t[:, i, :], Buf('pA%d' % i, psum=True)) for i in range(4)]
        s.pY = P.ps('pY', [128, 2, 512], F32)
        s.pYs = [(s.pY.t[:, :, :], [s.pY.b]), (pall.t[:, 2:4, :], [s.pA[2].b, s.pA[3].b])]
        s.alim = 4
        s.pT = [P.ps('pT%d' % i, [128, D], BF16) for i in range(2)]
        s.ai = 0
        s.si = 0
        s.ti = 0
        s.early = 0
        P.dma('sp', s.gp[:, :], gpost_ap, [], [s.gp.b], s.gp.b)
        P.dma('sp', s.ident[:, :], ident_ap, [], [s.ident.b], s.ident.b)
        P.op('pool', lambda e: e.memset(s.mhalf.t[:, :], -0.5), [], [s.mhalf.b])
        P.op('dve', lambda e: e.tensor_scalar(out=s.gp.t[:, :], in0=s.gp.t[:, :], scalar1=0.5, scalar2=None,
                                              op0=ALU.mult), [s.gp.b], [s.gp.b])

    def nextA(s):
        t = s.pA[s.ai % s.alim]
        s.ai += 1
        return t

    def nstat(s):
        t = s.stat[s.si % 8]
        s.si += 1
        return t

    def norm_pre(s, src_ap, src_b, xb):
        P = s.P
        st = s.nstat()
        jk = s.junk
        P.op('act', lambda e: e.activation(out=jk.t[:, :], in_=src_ap, func=AF.Square, accum_out=st.t[:, 0:1]),
             [src_b], [jk.b, st.b])
        rstd_from_ss(P, (st.t[:, 0:1], st.b), (st.t[:, 2:3], st.b), D, (s.mhalf.t[:, 0:1], s.mhalf.b),
                     (st.t[:, 1:2], st.b))
        P.op('dve', lambda e: e.tensor_scalar(out=xb.t[:, :], in0=src_ap, scalar1=st.t[:, 2:3], scalar2=None,
                                              op0=ALU.mult), [src_b, st.b], [xb.b])

    def norm_tr(s, xb, dstT, sub, pt):
        P = s.P
        for c in range(8):
            P.op('pe', lambda e, c=c: e.transpose(out=pt.t[:, c * 128:(c + 1) * 128], in_=xb.t[:, c * 128:(c + 1) * 128],
                                                  identity=s.ident.t[:, :]), [xb.b, s.ident.b], [pt.b])
        P.op('dve', lambda e: e.tensor_copy(out=dstT.t[:, :, sub * 128:(sub + 1) * 128],
                                            in_=pt.t[:, :].rearrange("p (c n) -> p c n", c=8)), [pt.b], [dstT.b])

    def load_x(s, x_src, buf):
        xi = s.xin[buf]
        s.P.dma('sp', xi.t[:, :, :], x_src.rearrange("(s p) d -> p s d", p=128), [], [xi.b], xi.b)

    def first(s, x_src):
        s.load_x(x_src, 0)
        xi = s.xin[0]
        for sub in range(4):
            xb = s.xnbx[sub % 2]
            s.norm_pre(xi.t[:, sub, :], xi.b, xb)
            s.norm_tr(xb, s.xnT, sub, s.pT[1])

    def run_tile(s, ring, next_src, cb_pre, cb_T, cb_mm=None):
        P = s.P
        ti = s.ti
        s.ti += 1
        xi = s.xin[ti % 2]
        xn = s.xin[(ti + 1) % 2]
        xT = s.xnT
        if next_src is not None:
            s.load_x(next_src, (ti + 1) % 2)
        def up(f):
            w1 = ring.next(('u', f, 0))
            w3 = ring.next(('u', f, 1))
            h1 = s.nextA()
            h3 = s.nextA()
            for (w, h) in [(w1, h1), (w3, h3)]:
                for c in range(8):
                    P.op('pe', lambda e, w=w, h=h, c=c: e.matmul(h.t[:, :], lhsT=w.t[:, c * 128:(c + 1) * 128],
                                                                  rhs=xT.t[:, c, :], start=(c == 0), stop=(c == 7)),
                         [w.b, xT.b], [h.b])
            ring.release(2)
            sl = s.sil[f % 2]
            P.op('act', lambda e, h1=h1, sl=sl: e.activation(out=sl.t[:, :], in_=h1.t[:, :], func=AF.Silu), [h1.b],
                 [sl.b])
            P.op('dve', lambda e, h3=h3, sl=sl, f=f: e.tensor_tensor(out=s.actT.t[:, f, :], in0=sl.t[:, :],
                                                                    in1=h3.t[:, :], op=ALU.mult), [sl.b, h3.b],
                 [s.actT.b])
        for f in range(s.early, NF):
            up(f)
        s.early = 0
        xbs = {}
        s.alim = 2
        for sub in range(4):
            py, pyb = s.pYs[sub % 2]
            if next_src is not None:
                xbs[sub] = s.xnbx[sub % 2]
                s.norm_pre(xn.t[:, sub, :], xn.b, xbs[sub])
            for half in range(2):
                for f in range(NF):
                    P.op('pe', lambda e, f=f, half=half, sub=sub: e.matmul(
                        py[:, half, :], lhsT=s.actT.t[:, f, sub * 128:(sub + 1) * 128],
                        rhs=s.w2.t[:, f, half * 512:(half + 1) * 512], start=(f == 0), stop=(f == NF - 1)),
                        [s.actT.b, s.w2.b], pyb)
            if sub >= 1:
                cb_T(sub - 1)
                if next_src is not None:
                    s.norm_tr(xbs[sub - 1], xT, sub - 1, s.pT[1])
                    if sub == 3:
                        s.norm_tr(xbs[3], xT, 3, s.pT[1])
            st = s.nstat()
            jk = s.junk2
            P.op('act', lambda e, st=st: e.activation(out=jk.t[:, :].rearrange("p (a n) -> p a n", a=2), in_=py,
                                                      func=AF.Square, accum_out=st.t[:, 0:1]), pyb, [jk.b, st.b])
            rstd_from_ss(P, (st.t[:, 0:1], st.b), (st.t[:, 2:3], st.b), D, (s.mhalf.t[:, 0:1], s.mhalf.b),
                         (st.t[:, 1:2], st.b))
            ho = s.hout[sub % s.nhout]
            P.op('dve', lambda e, st=st, ho=ho: e.scalar_tensor_tensor(
                out=ho.t[:, :].rearrange("p (a n) -> p a n", a=2), in0=py, scalar=st.t[:, 2:3],
                in1=s.gp.t[:, :].rearrange("p (a n) -> p a n", a=2), op0=ALU.mult, op1=ALU.mult),
                pyb + [st.b, s.gp.b], [ho.b])
            P.op('dve', lambda e, ho=ho, sub=sub: e.tensor_tensor(out=ho.t[:, :], in0=ho.t[:, :],
                                                                 in1=xi.t[:, sub, :], op=ALU.add),
                 [ho.b, xi.b], [ho.b])
            cb_pre(sub, ho)
            if sub >= 2 and cb_mm is not None:
                cb_mm(sub - 2)
        if next_src is not None:
            for f in range(EARLY):
                up(f)
            s.early = EARLY
        cb_T(3)
        if cb_mm is not None:
            cb_mm(2)
            cb_mm(3)
        s.alim = 4


def phase1(nc, P, cfg, I, S, DB):
    P.begin_phase()
    tiles = []
    for j in cfg.jobs:
        for t0 in range(0, j['ntok'], 512):
            own = (t0 >= j['q0']) and (t0 < j['q0'] + j['nq'])
            tiles.append((j, t0, own))
    plan = []
    for ti_, (j, t0, own) in enumerate(tiles):
        plan += ffn_plan(S['wffn1'], EARLY if ti_ > 0 else 0, NF)
        if ti_ + 1 < len(tiles):
            plan += ffn_plan(S['wffn1'], 0, EARLY)
        nsl = 29 if own else 6
        for k in range(nsl):
            plan.append((('in', k), S['win'][k], 1024))
        if own:
            for k in range(16):
                plan.append((('uq', k), S['wuq'][k], 384))
    ring = Ring(P, plan, n=12)
    fc = FfnCtx(nc, P, I['gpost'][:, 0:D], I['ident'], S['wffn1'])
    j0, t00, _ = tiles[0]
    fc.first(I['x_all'][j0['tb'] + t00:j0['tb'] + t00 + 512, :])
    uT = [P.sb('uT%d' % i, [128, 8, 512], BF16) for i in range(1)]
    ones = P.sb('onesb', [128, 128], BF16)
    P.op('pool', lambda e: e.memset(ones.t[:, :], 1.0), [], [ones.b])
    cqT = P.sb('cqT', [128, 3, 512], BF16)
    sq = [P.sb('sq%d' % i, [128, 512], BF16) for i in range(3)]
    ckvT = P.sb('ckvT', [128, 2, 512], F32)
    rbc = [P.sb('rbc%d' % i, [128, 512], F32) for i in range(2)]
    rtmp = P.sb('rtmp', [128, 512], F32)
    tabC = P.sb('tabC', [96, 512], F32, dma=True)
    tabS = P.sb('tabS', [96, 512], F32, dma=True)
    CR = P.sb('CR', [96, 512], F32)
    SR = P.sb('SR', [96, 512], F32)
    t1 = [P.sb('t1_%d' % i, [96, 512], F32) for i in range(1)]
    t2 = [P.sb('t2_%d' % i, [96, 512], F32) for i in range(1)]
    qst = [P.sb('qst%d' % i, [96, 512], BF16, dma='sw') for i in range(3)]
    ckvn = P.sb('ckvn', [128, 2, 512], BF16, dma='sw')
    krs_sb = P.sb('krs_sb', [96, 512], F32)
    kro = P.sb('kro', [96, 512], BF16, dma='sw')
    kbo = P.sb('kbo', [64, 2, 512], BF16, dma='sw')
    vbo = P.sb('vbo', [128, 4, 128], BF16, dma='sw')
    qbo = Tile(fc.actT.t[0:64, 14:22, :], fc.actT.b)
    sgo = [Tile(fc.actT.t[:, 6:14, :], fc.actT.b)]
    QSCALE = 96.0 ** -0.5

    for ti, (j, t0, own) in enumerate(tiles):
        g0 = j['tb'] + t0
        nsrc = None
        if ti + 1 < len(tiles):
            jn, tn, _ = tiles[ti + 1]
            nsrc = I['x_all'][jn['tb'] + tn:jn['tb'] + tn + 512, :]
        u = uT[0]
        uxb = {}

        def cb_pre(sub, ho, j=j, t0=t0, own=own):
            if own:
                q0g_ = j['qb'] + (t0 - j['q0']) + sub * 128
                P.dma('pool', S['hsp'][q0g_:q0g_ + 128, :], ho.t[:, :], [ho.b], [DB['hsp']], ho.b)
            uxb[sub] = fc.xnb[sub % 2]
            fc.norm_pre(ho.t[:, :], ho.b, uxb[sub])

        def cb_T(sub, u=u):
            fc.norm_tr(uxb[sub], u, sub, fc.pT[0])
        fc.run_tile(ring, nsrc, cb_pre, cb_T)
        CUT = getattr(cfg, 'cut', 99)

        def drain(kfrom, own=own):
            for k in range(kfrom, 29 if own else 6):
                ring.next(('in', k))
                ring.release(1)
            if own:
                for k in range(16):
                    ring.next(('uq', k))
                    ring.release(1)
        if CUT <= 1:
            drain(0)
            continue
        P.dma('sp', tabC[:, :], I['ropeC'][:, g0:g0 + 512], [], [tabC.b], tabC.b)
        P.dma('sp', tabS[:, :], I['ropeS'][:, g0:g0 + 512], [], [tabS.b], tabS.b)

        def lin(key, ncols, col0=0):
            w = ring.next(key)
            pp = fc.nextA()
            for c in range(8):
                P.op('pe', lambda e, w=w, pp=pp, c=c: e.matmul(pp.t[0:ncols, :],
                                                                lhsT=w.t[:, c * 128 + col0:c * 128 + col0 + ncols],
                                                                rhs=u.t[:, c, :], start=(c == 0), stop=(c == 7)),
                     [w.b, u.b], [pp.b])
            return w, pp

        for c2 in range(2):
            w, pp = lin(('in', c2), 128)
            ring.release(1)
            P.op('dve', lambda e, pp=pp, c2=c2: e.tensor_copy(out=ckvT.t[:, c2, :], in_=pp.t[:, :]), [pp.b], [ckvT.b])
            P.op('act', lambda e, pp=pp, c2=c2: e.activation(out=sq[c2].t[:, :], in_=pp.t[:, :], func=AF.Square),
                 [pp.b], [sq[c2].b])
        pss = fc.nextA()
        for c2 in range(2):
            P.op('pe', lambda e, c2=c2: e.matmul(pss.t[:, :], lhsT=ones.t[:, :], rhs=sq[c2].t[:, :], start=(c2 == 0),
                                                 stop=(c2 == 1)), [ones.b, sq[c2].b], [pss.b])
        rk = rbc[0]
        rstd_big(P, pss, rk, 256, rtmp)
        for c2 in range(2):
            P.op('dve', lambda e, c2=c2: e.tensor_tensor(out=ckvn.t[:, c2, :], in0=ckvT.t[:, c2, :], in1=rk.t[:, :],
                                                         op=ALU.mult), [ckvT.b, rk.b], [ckvn.b])
        P.dma('pool', S['ckvnT'].rearrange("(c p) n -> p c n", p=128)[:, :, g0:g0 + 512], ckvn.t[:, :, :],
              [ckvn.b], [DB['ckvnT']], ckvn.b)
        if CUT <= 2:
            drain(2)
            continue
        w, pk = lin(('in', 2), 96)
        ring.release(1)
        w, pks = lin(('in', 3), 96)
        ring.release(1)
        P.op('dve', lambda e: e.tensor_tensor(out=krs_sb.t[64:96, :], in0=pks.t[64:96, :], in1=tabS.t[64:96, :],
                                              op=ALU.mult), [pks.b, tabS.b], [krs_sb.b])
        P.op('dve', lambda e: e.tensor_tensor(out=t1[0].t[64:96, :], in0=pk.t[64:96, :], in1=tabC.t[64:96, :],
                                              op=ALU.mult), [pk.b, tabC.b], [t1[0].b])
        P.op('pool', lambda e: e.tensor_tensor(out=kro.t[64:96, :], in0=t1[0].t[64:96, :], in1=krs_sb.t[64:96, :],
                                               op=ALU.add), [t1[0].b, krs_sb.b], [kro.b])
        P.dma('pool', S['kropeT'][:, g0:g0 + 512], kro.t[64:96, :], [kro.b], [DB['kropeT']], kro.b)
        if CUT <= 3:
            drain(4)
            continue
        w = ring.next(('in', 4))
        for g in range(2):
            pp = fc.nextA()
            for c in range(8):
                P.op('pe', lambda e, w=w, pp=pp, c=c, g=g: e.matmul(pp.t[0:64, :],
                                                                    lhsT=w.t[:, c * 128 + g * 64:c * 128 + g * 64 + 64],
                                                                    rhs=u.t[:, c, :], start=(c == 0), stop=(c == 7)),
                     [w.b, u.b], [pp.b])
            P.op('act', lambda e, pp=pp, g=g: e.activation(out=kbo.t[:, g, :], in_=pp.t[0:64, :], func=AF.Copy),
                 [pp.b], [kbo.b])
        ring.release(1)
        P.dma('pool', S['kbT'].rearrange("g p n -> p g n")[:, :, g0:g0 + 512], kbo.t[:, :, :], [kbo.b], [DB['kbT']],
              kbo.b)
        w = ring.next(('in', 5))
        pv = fc.nextA()
        for sub in range(4):
            for c in range(8):
                P.op('pe', lambda e, w=w, sub=sub, c=c: e.matmul(pv.t[:, sub * 128:(sub + 1) * 128],
                                                                 lhsT=u.t[:, c, sub * 128:(sub + 1) * 128],
                                                                 rhs=w.t[:, c * 128:(c + 1) * 128], start=(c == 0),
                                                                 stop=(c == 7)), [w.b, u.b], [pv.b])
        ring.release(1)
        P.op('dve', lambda e: e.tensor_copy(out=vbo.t[:, :, :], in_=pv.t[:, :].rearrange("p (s n) -> p s n", s=4)),
             [pv.b], [vbo.b])
        P.dma('pool', S['vbs'][g0:g0 + 512, :].rearrange("(s p) n -> p s n", p=128), vbo.t[:, :, :], [vbo.b],
              [DB['vbs']], vbo.b)
        if not own:
            continue
        if CUT <= 4:
            drain(6)
            continue
        q0g = j['qb'] + (t0 - j['q0'])
        for c3 in range(3):
            w, pp = lin(('in', 6 + c3), 128)
            ring.release(1)
            P.op('dve', lambda e, pp=pp, c3=c3: e.tensor_copy(out=cqT.t[:, c3, :], in_=pp.t[:, :]), [pp.b], [cqT.b])
            P.op('act', lambda e, pp=pp, c3=c3: e.activation(out=sq[c3].t[:, :], in_=pp.t[:, :], func=AF.Square),
                 [pp.b], [sq[c3].b])
        pss = fc.nextA()
        for c3 in range(3):
            P.op('pe', lambda e, c3=c3: e.matmul(pss.t[:, :], lhsT=ones.t[:, :], rhs=sq[c3].t[:, :], start=(c3 == 0),
                                                 stop=(c3 == 2)), [ones.b, sq[c3].b], [pss.b])
        rq = rbc[1]
        rstd_big(P, pss, rq, 384, rtmp)
        P.op('dve', lambda e: e.scalar_tensor_tensor(out=CR.t[:, :], in0=tabC.t[:, :], scalar=QSCALE,
                                                     in1=rq.t[0:96, :], op0=ALU.mult, op1=ALU.mult),
             [tabC.b, rq.b], [CR.b])
        P.op('dve', lambda e: e.scalar_tensor_tensor(out=SR.t[:, :], in0=tabS.t[:, :], scalar=QSCALE,
                                                     in1=rq.t[0:96, :], op0=ALU.mult, op1=ALU.mult),
             [tabS.b, rq.b], [SR.b])
        if CUT <= 5:
            drain(9)
            continue
        for k in range(4):
            w = ring.next(('in', 9 + k))
            for hh in range(2):
                h = 2 * k + hh
                pp = fc.nextA()
                for c in range(8):
                    P.op('pe', lambda e, w=w, pp=pp, c=c, hh=hh: e.matmul(
                        pp.t[0:64, :], lhsT=w.t[:, c * 128 + hh * 64:c * 128 + hh * 64 + 64], rhs=u.t[:, c, :],
                        start=(c == 0), stop=(c == 7)), [w.b, u.b], [pp.b])
                P.op('act', lambda e, pp=pp, h=h: e.activation(out=qbo.t[:, h, :], in_=pp.t[0:64, :], func=AF.Copy,
                                                               scale=0.125), [pp.b], [qbo.b])
            ring.release(1)
        P.dma('pool', S['qbT'].rearrange("h p n -> p h n")[:, :, q0g:q0g + 512], qbo.t, [qbo.b], [DB['qbT']],
              qbo.b)
        if CUT <= 6:
            drain(13)
            continue
        for gi, nm in enumerate(['sga', 'sgb']):
            so = sgo[0]
            for k in range(8):
                w, pp = lin(('in', 13 + gi * 8 + k), 128)
                ring.release(1)
                P.op('act', lambda e, pp=pp, k=k, so=so: e.activation(out=so.t[:, k, :], in_=pp.t[:, :],
                                                                       func=AF.Sigmoid), [pp.b], [so.b])
            P.dma('pool', S[nm].rearrange("k p n -> p k n")[:, :, q0g:q0g + 512], so.t, [so.b], [DB[nm]],
                  so.b)
        if CUT <= 7:
            drain(29)
            continue
        for h in range(8):
            wr = ring.next(('uq', 2 * h))
            ws = ring.next(('uq', 2 * h + 1))
            pr = fc.nextA()
            pw = fc.nextA()
            for (w, pp) in [(wr, pr), (ws, pw)]:
                for c3 in range(3):
                    P.op('pe', lambda e, w=w, pp=pp, c3=c3: e.matmul(pp.t[0:96, :], lhsT=w.t[:, c3 * 128:c3 * 128 + 96],
                                                                      rhs=cqT.t[:, c3, :], start=(c3 == 0),
                                                                      stop=(c3 == 2)), [w.b, cqT.b], [pp.b])
            ring.release(2)
            a = t1[0]
            b = t2[0]
            P.op('dve', lambda e, pr=pr, a=a: e.tensor_tensor(out=a.t[:, :], in0=pr.t[0:96, :], in1=CR.t[:, :],
                                                              op=ALU.mult), [pr.b, CR.b], [a.b])
            P.op('dve', lambda e, pw=pw, b=b: e.tensor_tensor(out=b.t[:, :], in0=pw.t[0:96, :], in1=SR.t[:, :],
                                                              op=ALU.mult), [pw.b, SR.b], [b.b])
            qs = qst[h % 3]
            P.op('pool', lambda e, a=a, b=b, qs=qs: e.tensor_tensor(out=qs.t[:, :], in0=a.t[:, :], in1=b.t[:, :],
                                                                     op=ALU.add), [a.b, b.b], [qs.b])
            P.dma('pool', S['QT'][h][:, q0g:q0g + 512], qs.t[:, :], [qs.b], [DB['QT']], qs.b)
    P.end_phase()


def phase2a(nc, P, cfg, I, S, DB):
    P.begin_phase()
    maxk = max(j['ntok'] for j in cfg.jobs)
    wuk = P.sb('wuk', [128, 2, 512], BF16, dma=True)
    wuv = P.sb('wuv', [128, 2, 512], BF16, dma=True)
    P.dma('sp', wuk.t[:, :, :], S['wukv'][0].rearrange("p (c n) -> p c n", c=2), [DB['wukv']], [wuk.b], wuk.b)
    P.dma('sp', wuv.t[:, :, :], S['wukv'][1].rearrange("p (c n) -> p c n", c=2), [DB['wukv']], [wuv.b], wuv.b)
    shf = P.sb('shf', [128, 64], F32, dma=True)
    P.dma('sp', shf.t[:, :], I['shift'], [], [shf.b], shf.b)
    KT = [P.sb('KT%d' % i, [96, maxk], BF16, dma=True) for i in range(2)]
    VA = [P.sb('VA%d' % i, [128, maxk // 128, 128], BF16) for i in range(2)]
    for i in range(2):
        P.op('pool', lambda e, i=i: e.memset(VA[i].t[:, :, 64:128], 1.0), [], [VA[i].b])
    ck = [P.sb('ck%d' % i, [128, 2, 512], BF16, dma=True) for i in range(2)]
    qt = [P.sb('qt%d' % i, [96, 512], BF16, dma=True) for i in range(2)]
    PT = [P.sb('PT%d' % i, [128, 1024], BF16) for i in range(3)]
    osb = [P.sb('osb%d' % i, [128, 512], F32) for i in range(2)]
    yst = [P.sb('yst%d' % i, [64, 512], BF16, dma='sw') for i in range(2)]
    pS = [P.ps('pS%d' % i, [128, 1024], F32) for i in range(3)]
    pO = P.ps('pO', [128, 512], F32)
    pB = P.ps('pB', [128, 512], F32)
    cnt = dict(ck=0, qt=0, o=0, y=0)

    def build_kv(j, h, buf):
        kt, va = KT[buf], VA[buf]
        tb, nk = j['tb'], j['ntok']
        for k0 in range(0, nk, 512):
            c = ck[cnt['ck'] % 2]
            cnt['ck'] += 1
            P.dma('sp', c.t[:, :, :], S['ckvnT'].rearrange("(c p) n -> p c n", p=128)[:, :, tb + k0:tb + k0 + 512],
                  [DB['ckvnT']], [c.b], c.b)
            for c2 in range(2):
                P.op('pe', lambda e, c2=c2: e.matmul(pB.t[0:64, :], lhsT=wuk.t[:, c2, h * 64:(h + 1) * 64],
                                                     rhs=c.t[:, c2, :], start=(c2 == 0), stop=(c2 == 1)),
                     [wuk.b, c.b], [pB.b])
            P.op('dve', lambda e: e.tensor_copy(out=kt.t[0:64, k0:k0 + 512], in_=pB.t[0:64, :]), [pB.b], [kt.b])
            yield
            for s4 in range(4):
                for c2 in range(2):
                    P.op('pe', lambda e, c2=c2, s4=s4: e.matmul(pB.t[:, s4 * 64:(s4 + 1) * 64],
                                                                 lhsT=c.t[:, c2, s4 * 128:(s4 + 1) * 128],
                                                                 rhs=wuv.t[:, c2, h * 64:(h + 1) * 64],
                                                                 start=(c2 == 0), stop=(c2 == 1)),
                         [wuv.b, c.b], [pB.b])
            P.op('dve', lambda e: e.tensor_copy(out=va.t[:, k0 // 128:k0 // 128 + 4, 0:64],
                                                in_=pB.t[:, 0:256].rearrange("p (s n) -> p s n", s=4)),
                 [pB.b], [va.b])
            yield

    def drain(gen):
        if gen is not None:
            for _ in gen:
                pass

    for j in cfg.jobs:
        tb, nk, nq, q0, qb = j['tb'], j['ntok'], j['nq'], j['q0'], j['qb']
        for i in range(2):
            P.dma('sp', KT[i].t[64:96, 0:nk], S['kropeT'][:, tb:tb + nk], [DB['kropeT']], [KT[i].b], KT[i].b)
        drain(build_kv(j, 0, 0))
        nsb = nk // 256
        nqt = nq // 512
        stride = max(1, (nsb * nqt - 2) // (2 * (nk // 512)))
        pending = [None]
        for h in range(8):
            kt, va = KT[h % 2], VA[h % 2]
            gen = build_kv(j, h + 1, (h + 1) % 2) if h + 1 < 8 else None
            gcount = 0
            for qi in range(nqt):
                q = qt[cnt['qt'] % 2]
                cnt['qt'] += 1
                P.dma('sp', q.t[:, :], S['QT'][h][:, qb + qi * 512:qb + (qi + 1) * 512], [DB['QT']], [q.b], q.b)
                po = pO

                def qk(sbi):
                    ps_ = pS[sbi % 3]
                    for t in range(2):
                        k0 = (2 * sbi + t) * 128
                        P.op('pe', lambda e, t=t, k0=k0: e.matmul(ps_.t[:, t * 512:(t + 1) * 512],
                                                                  lhsT=kt.t[:, k0:k0 + 128], rhs=q.t[:, :],
                                                                  start=True, stop=True), [kt.b, q.b], [ps_.b])
                    pt_ = PT[sbi % 3]
                    P.op('act', lambda e: e.activation(out=pt_.t[:, :], in_=ps_.t[:, :], func=AF.Exp), [ps_.b], [pt_.b])

                def pv(sbi):
                    pt_ = PT[sbi % 3]
                    for t in range(2):
                        kk = 2 * sbi + t
                        P.op('pe', lambda e, t=t, kk=kk: e.matmul(po.t[:, :], lhsT=va.t[:, kk, :],
                                                                  rhs=pt_.t[:, t * 512:(t + 1) * 512],
                                                                  start=(kk == 0), stop=(kk == 2 * nsb - 1)),
                             [va.b, pt_.b], [po.b])
                qk(0)
                if nsb > 1:
                    qk(1)
                for sbi in range(nsb):
                    if sbi + 2 < nsb:
                        qk(sbi + 2)
                    pv(sbi)
                    if sbi == 1 and pending[0] is not None:
                        pending[0]()
                        pending[0] = None
                    gcount += 1
                    if gen is not None and gcount % stride == 0:
                        next(gen, None)
                ob = osb[cnt['o'] % 2]
                cnt['o'] += 1
                P.op('dve', lambda e: e.tensor_copy(out=ob.t[0:64, :], in_=po.t[0:64, :]), [po.b], [ob.b])
                P.op('dve', lambda e: e.reciprocal(out=ob.t[64:128, :], in_=po.t[64:128, :]), [po.b], [ob.b])

                def part2(ob=ob, h=h, qi=qi, qb=qb):
                    P.op('pe', lambda e: e.matmul(pB.t[0:64, :], lhsT=shf.t[64:128, :], rhs=ob.t[64:128, :],
                                                  start=True, stop=True), [shf.b, ob.b], [pB.b])
                    ys = yst[cnt['y'] % 2]
                    cnt['y'] += 1
                    P.op('dve', lambda e: e.tensor_tensor(out=ys.t[:, :], in0=ob.t[0:64, :], in1=pB.t[0:64, :],
                                                          op=ALU.mult), [ob.b, pB.b], [ys.b])
                    P.dma('pool', S['yaT'][h][:, qb + qi * 512:qb + (qi + 1) * 512], ys.t[:, :], [ys.b], [DB['yaT']],
                          ys.b)
                if pending[0] is not None:
                    pending[0]()
                pending[0] = part2
            drain(gen)
        if pending[0] is not None:
            pending[0]()
            pending[0] = None
    P.end_phase()


def phase2b(nc, P, cfg, I, S, DB):
    P.begin_phase()
    bias = P.sb('bias', [128, 3, 8, 128], F32, dma=True)
    bsrc = S['biasD'].rearrange("h (k j i) -> j k h i", k=3, j=128)
    P.dmas('sp', [(bias.t[:, k, :, :], bsrc[:, k, :, :]) for k in range(3)], [DB['biasD']], [bias.b], bias.b)
    edge = P.sb('edge', [128, 2], F32, dma=True)
    P.dma('sp', edge.t[:, :], I['edge'], [], [edge.b], edge.b)
    snk = P.sb('snk', [128, 8], F32, dma=True)
    P.dma('sp', snk.t[:, :], I['sinkb'], [], [snk.b], snk.b)
    ident = P.sb('identw', [128, 128], BF16, dma=True)
    P.dma('sp', ident.t[:, :], I['ident'], [], [ident.b], ident.b)
    esk = P.sb('esk', [128, 8], F32)
    P.op('act', lambda e: e.activation(out=esk.t[:, :], in_=snk.t[:, :], func=AF.Exp), [snk.b], [esk.b])
    qbt = [P.sb('qbt%d' % i, [64, 8, 512], BF16, dma=True) for i in range(2)]
    kbt = [P.sb('kbt%d' % i, [64, 2, 768], BF16, dma=True) for i in range(2)]
    vbt = [P.sb('vbt%d' % i, [128, 6, 2, 128], BF16, dma=True) for i in range(2)]
    for i in range(2):
        P.op('pool', lambda e, i=i: e.memset(vbt[i].t[:, :, :, 64:128], 1.0), [], [vbt[i].b])
    sbs = [P.sb('sbs%d' % i, [128, 3, 512], F32) for i in range(2)]
    pt3 = [P.sb('pt3_%d' % i, [128, 3, 512], BF16) for i in range(4)]
    den8 = [P.sb('den8_%d' % i, [128, 8], F32) for i in range(2)]
    ytm = [P.sb('ytm%d' % i, [128, 8, 64], BF16) for i in range(2)]
    ybo = [P.sb('ybo%d' % i, [128, 4, 512], BF16, dma='sw') for i in range(2)]
    pS3 = [P.ps('pS3_%d' % i, [128, 3, 512], F32) for i in range(2)]
    pVg = [P.ps('pVw%d' % i, [128, 4, 128], F32) for i in range(2)]
    it = 0
    ybdst = S['ybT'].rearrange("(hp h2) p n -> hp (h2 p) n", h2=2)
    for j in cfg.jobs:
        tb, nk, nq, q0, qb = j['tb'], j['ntok'], j['nq'], j['q0'], j['qb']
        for ti in range(nq // 512):
            qs = q0 + ti * 512
            klo = max(0, qs - 128)
            khi = min(nk, qs + 512 + 128)
            qt_ = qbt[ti % 2]
            kt_ = kbt[ti % 2]
            vt_ = vbt[ti % 2]
            yo = ybo[ti % 2]
            P.dma('sp', qt_.t[:, :, :], S['qbT'].rearrange("h p n -> p h n")[:, :, qb + ti * 512:qb + (ti + 1) * 512],
                  [DB['qbT']], [qt_.b], qt_.b)
            off = klo - (qs - 128)
            P.dma('sp', kt_.t[:, :, off:off + (khi - klo)],
                  S['kbT'].rearrange("g p n -> p g n")[:, :, tb + klo:tb + khi], [DB['kbT']], [kt_.b], kt_.b)
            P.dmas('sp', [(vt_.t[:, off // 128:(off + khi - klo) // 128, g, 0:64],
                           S['vbs'][tb + klo:tb + khi, g * 64:(g + 1) * 64].rearrange("(s p) n -> p s n", p=128))
                          for g in range(2)], [DB['vbs']], [vt_.b], vt_.b)
            def kbs_of(ql):
                qtok = qs + ql * 128
                return [kb for kb in range(3) if 0 <= qtok + (kb - 1) * 128 < nk]

            def stage1(ql):
                kbs = kbs_of(ql)
                k0, k1 = kbs[0], kbs[-1] + 1
                for g in range(2):
                    ps_ = pS3[g]
                    sb_ = sbs[g]
                    p3 = pt3[(2 * ql + g) % 4]
                    for kb in kbs:
                        col = (ql + kb) * 128
                        P.op('pe', lambda e, kb=kb, col=col: e.matmul(
                            ps_.t[:, kb, :], lhsT=kt_.t[:, g, col:col + 128],
                            rhs=qt_.t[:, g * 4:(g + 1) * 4, ql * 128:(ql + 1) * 128], start=True, stop=True),
                            [kt_.b, qt_.b], [ps_.b])
                    P.op('dve', lambda e: e.tensor_tensor(
                        out=sb_.t[:, k0:k1, :].rearrange("p k (h i) -> p k h i", h=4),
                        in0=ps_.t[:, k0:k1, :].rearrange("p k (h i) -> p k h i", h=4),
                        in1=bias.t[:, k0:k1, g * 4:(g + 1) * 4, :], op=ALU.add), [ps_.b, bias.b], [sb_.b])
                    if j['sample'] and ti == 0 and ql == 0:
                        P.op('dve', lambda e: e.tensor_scalar(out=sb_.t[:, 0, :], in0=sb_.t[:, 0, :],
                                                              scalar1=edge.t[:, 0:1], scalar2=None, op0=ALU.add),
                             [sb_.b, edge.b], [sb_.b])
                    if j['sample'] and ti == nq // 512 - 1 and ql == 3:
                        P.op('dve', lambda e: e.tensor_scalar(out=sb_.t[:, 2, :], in0=sb_.t[:, 2, :],
                                                              scalar1=edge.t[:, 1:2], scalar2=None, op0=ALU.add),
                             [sb_.b, edge.b], [sb_.b])
                    P.op('act', lambda e: e.activation(out=p3.t[:, k0:k1, :], in_=sb_.t[:, k0:k1, :], func=AF.Exp),
                         [sb_.b], [p3.b])

            def stage2(ql):
                kbs = kbs_of(ql)
                for g in range(2):
                    p3 = pt3[(2 * ql + g) % 4]
                    pv_ = pVg[g]
                    for hh in range(4):
                        for kb in kbs:
                            P.op('pe', lambda e, kb=kb, hh=hh: e.matmul(
                                pv_.t[:, hh, 0:65], lhsT=p3.t[:, kb, hh * 128:(hh + 1) * 128],
                                rhs=vt_.t[:, ql + kb, g, 0:65], start=(kb == kbs[0]), stop=(kb == kbs[-1])),
                                [p3.b, vt_.b], [pv_.b])
                d8 = den8[ql % 2]
                for g in range(2):
                    P.op('dve', lambda e, g=g: e.tensor_tensor(out=d8.t[:, g * 4:(g + 1) * 4], in0=pVg[g].t[:, :, 64],
                                                               in1=esk.t[:, g * 4:(g + 1) * 4], op=ALU.add),
                         [pVg[g].b, esk.b], [d8.b])
                P.op('dve', lambda e: e.reciprocal(out=d8.t[:, :], in_=d8.t[:, :]), [d8.b], [d8.b])
                yt = ytm[ql % 2]
                for h in range(8):
                    pv_ = pVg[h // 4]
                    if h % 2 == 0:
                        P.op('act', lambda e, h=h, pv_=pv_: e.activation(out=yt.t[:, h, :], in_=pv_.t[:, h % 4, 0:64],
                                                                         func=AF.Copy, scale=d8.t[:, h:h + 1]),
                             [pv_.b, d8.b], [yt.b])
                    else:
                        P.op('dve', lambda e, h=h, pv_=pv_: e.tensor_scalar(out=yt.t[:, h, :], in0=pv_.t[:, h % 4, 0:64],
                                                                            scalar1=d8.t[:, h:h + 1], scalar2=None,
                                                                            op0=ALU.mult), [pv_.b, d8.b], [yt.b])
                pst = pS3[1]
                pTv = pst.t[:, 0, :].bitcast(BF16)
                for hp in range(4):
                    P.op('pe', lambda e, hp=hp: e.transpose(out=pTv[:, hp * 128:(hp + 1) * 128],
                                                            in_=yt.t[:, 2 * hp:2 * hp + 2, :].rearrange("p h d -> p (h d)"),
                                                            identity=ident.t[:, :]), [yt.b, ident.b], [pst.b])
                P.op('dve', lambda e: e.tensor_copy(out=yo.t[:, :, ql * 128:(ql + 1) * 128],
                                                    in_=pTv[:, 0:512].rearrange("p (hp i) -> p hp i", hp=4)),
                     [pst.b], [yo.b])

            stage1(0)
            for ql in range(4):
                if ql + 1 < 4:
                    stage1(ql + 1)
                stage2(ql)
            P.dma('pool', ybdst.rearrange("hp q n -> q hp n")[:, :, qb + ti * 512:qb + (ti + 1) * 512], yo.t[:, :, :],
                  [yo.b], [DB['ybT']], yo.b)
    P.end_phase()


def phase2c(nc, P, cfg, I, S, DB):
    P.begin_phase()
    wpa = P.sb('wpa', [128, 4, D], BF16, dma=True)
    wpb = P.sb('wpb', [128, 4, D], BF16, dma=True)
    for i_, w_ in enumerate([wpa, wpb]):
        src = S['wpab'][i_].rearrange("p (hp h2 n) -> p hp h2 n", hp=4, h2=2)
        P.dmas('sp', [(w_.t[h2 * 64:(h2 + 1) * 64, :, :], src[:, :, h2, :]) for h2 in range(2)], [DB['wpab']], [w_.b],
               w_.b)
    wo = P.sb('wo', [128, 8, D], BF16, dma=True)
    P.dma('sp', wo.t[:, :, :], S['wout'].rearrange("f p n -> p f n"), [DB['wout']], [wo.b], wo.b)
    gp = P.sb('gpm', [128, D], F32, dma=True)
    P.dma('sp', gp.t[:, :], I['gpost'][:, D:2 * D], [], [gp.b], gp.b)
    mhalf = P.sb('mhalf2', [128, 8], F32)
    P.op('pool', lambda e: e.memset(mhalf.t[:, :], -0.5), [], [mhalf.b])
    ya = [P.sb('ya%d' % i, [128, 4, 512], BF16, dma=True) for i in range(2)]
    yb = [P.sb('yb%d' % i, [128, 4, 512], BF16, dma=True) for i in range(2)]
    ga = [P.sb('ga%d' % i, [128, 8, 512], BF16, dma=True) for i in range(2)]
    gb = [P.sb('gb%d' % i, [128, 8, 512], BF16, dma=True) for i in range(2)]
    hin = [P.sb('hin%d' % i, [128, 4, D], F32, dma=True) for i in range(2)]
    mT = P.sb('mT', [128, 8, 512], BF16)
    ta = [P.sb('ta%d' % i, [128, 512], F32) for i in range(2)]
    tbb = [P.sb('tb%d' % i, [128, 512], F32) for i in range(2)]
    ty = [P.sb('tym%d' % i, [128, D], F32) for i in range(2)]
    h2o = [P.sb('h2o%d' % i, [128, D], F32, dma='sw') for i in range(2)]
    junk = P.sb('junkm', [128, D], BF16)
    stat = [P.sb('statm%d' % i, [128, 8], F32) for i in range(4)]
    pP = [P.ps('pP%d' % i, [128, 512], F32) for i in range(4)]
    pM = [P.ps('pM%d' % i, [128, D], F32) for i in range(2)]
    nt = cfg.NQ // 512
    for ti in range(nt):
        a, b_, g1, g2, hi = ya[ti % 2], yb[ti % 2], ga[ti % 2], gb[ti % 2], hin[ti % 2]
        c0 = ti * 512
        for (t_, nm_) in [(a, 'yaT'), (b_, 'ybT')]:
            src = S[nm_].rearrange("(hp h2) p n -> p hp h2 n", h2=2)
            P.dmas('sp', [(t_.t[h2 * 64:(h2 + 1) * 64, :, :], src[:, :, h2, c0:c0 + 512]) for h2 in range(2)],
                   [DB[nm_]], [t_.b], t_.b)
        P.dma('sp', g1.t[:, :, :], S['sga'].rearrange("k p n -> p k n")[:, :, c0:c0 + 512], [DB['sga']], [g1.b], g1.b)
        P.dma('sp', g2.t[:, :, :], S['sgb'].rearrange("k p n -> p k n")[:, :, c0:c0 + 512], [DB['sgb']], [g2.b], g2.b)
        P.dma('sp', hi.t[:, :, :], S['hsp'][c0:c0 + 512, :].rearrange("(s p) d -> p s d", p=128), [DB['hsp']], [hi.b],
              hi.b)
        for k in range(8):
            pa = pP[(2 * k) % 4]
            pb = pP[(2 * k + 1) % 4]
            for (w, y, pp) in [(wpa, a, pa), (wpb, b_, pb)]:
                for h in range(4):
                    P.op('pe', lambda e, w=w, y=y, pp=pp, h=h, k=k: e.matmul(pp.t[:, :],
                                                                              lhsT=w.t[:, h, k * 128:(k + 1) * 128],
                                                                              rhs=y.t[:, h, :], start=(h == 0),
                                                                              stop=(h == 3)), [w.b, y.b], [pp.b])
            x1 = ta[k % 2]
            x2 = tbb[k % 2]
            P.op('dve', lambda e, pa=pa, x1=x1, k=k: e.tensor_tensor(out=x1.t[:, :], in0=pa.t[:, :], in1=g1.t[:, k, :],
                                                                     op=ALU.mult), [pa.b, g1.b], [x1.b])
            P.op('dve', lambda e, pb=pb, x2=x2, k=k: e.tensor_tensor(out=x2.t[:, :], in0=pb.t[:, :], in1=g2.t[:, k, :],
                                                                     op=ALU.mult), [pb.b, g2.b], [x2.b])
            P.op('pool', lambda e, x1=x1, x2=x2, k=k: e.tensor_tensor(out=mT.t[:, k, :], in0=x1.t[:, :], in1=x2.t[:, :],
                                                                      op=ALU.add), [x1.b, x2.b], [mT.b])
        for sub in range(4):
            pm = pM[sub % 2]
            for half in range(2):
                for k in range(8):
                    P.op('pe', lambda e, pm=pm, half=half, k=k, sub=sub: e.matmul(
                        pm.t[:, half * 512:(half + 1) * 512], lhsT=mT.t[:, k, sub * 128:(sub + 1) * 128],
                        rhs=wo.t[:, k, half * 512:(half + 1) * 512], start=(k == 0), stop=(k == 7)), [mT.b, wo.b],
                        [pm.b])
            st = stat[sub % 4]
            P.op('act', lambda e, pm=pm, st=st: e.activation(out=junk.t[:, :], in_=pm.t[:, :], func=AF.Square,
                                                             accum_out=st.t[:, 0:1]), [pm.b], [junk.b, st.b])
            rstd_from_ss(P, (st.t[:, 0:1], st.b), (st.t[:, 2:3], st.b), D, (mhalf.t[:, 0:1], mhalf.b),
                         (st.t[:, 1:2], st.b))
            t_ = ty[sub % 2]
            P.op('dve', lambda e, pm=pm, st=st, t_=t_: e.scalar_tensor_tensor(out=t_.t[:, :], in0=pm.t[:, :],
                                                                              scalar=st.t[:, 2:3], in1=gp.t[:, :],
                                                                              op0=ALU.mult, op1=ALU.mult),
                 [pm.b, st.b, gp.b], [t_.b])
            ho = h2o[sub % 2]
            P.op('pool', lambda e, t_=t_, ho=ho, sub=sub: e.tensor_tensor(out=ho.t[:, :], in0=t_.t[:, :],
                                                                         in1=hi.t[:, sub, :], op=ALU.add),
                 [t_.b, hi.b], [ho.b])
            r0 = c0 + sub * 128
            P.dma('pool', S['h2sp'][r0:r0 + 128, :], ho.t[:, :], [ho.b], [DB['h2sp']], ho.b)
    P.end_phase()


def phase3(nc, P, cfg, I, S, DB, y_out):
    P.begin_phase()
    nt = cfg.NQ // 512
    plan = []
    for ti in range(nt):
        plan += ffn_plan(S['wffn2'], EARLY if ti > 0 else 0, NF)
        if ti + 1 < nt:
            plan += ffn_plan(S['wffn2'], 0, EARLY)
    ring = Ring(P, plan, n=8)
    fc = FfnCtx(nc, P, I['gpost'][:, 2 * D:3 * D], I['ident'], S['wffn2'], nhout=4)
    gpe = P.sb('gpe', [128, D], F32, dma=True)
    P.dma('sp', gpe.t[:, :], I['gpost'][:, 3 * D:4 * D], [], [gpe.b], gpe.b)
    wpg = P.sb('wpg', [128, 8, D], BF16, dma=True)
    P.dma('sp', wpg.t[:, :, :], S['wpg'].rearrange("f p n -> p f n"), [DB['wpg']], [wpg.b], wpg.b)
    wpe = P.sb('wpe', [128, 2, D], BF16, dma=True)
    P.dma('sp', wpe.t[:, :, :], S['wpe'].rearrange("f p n -> p f n"), [DB['wpe']], [wpe.b], wpe.b)
    hT = P.sb('hT3', [128, 8, 512], BF16)
    pin = P.sb('pin', [128, 4, PLE], F32, dma=True)
    pbf = [P.sb('pbf%d' % i, [128, PLE], BF16) for i in range(2)]
    pT_sb = [P.sb('pTs%d' % i, [128, 2, 128], BF16) for i in range(2)]
    sg = P.sb('sg', [128, D], BF16)
    ev = [P.sb('ev%d' % i, [128, D], F32, dma='sw') for i in range(2)]
    fc.first(S['h2sp'][0:512, :])
    for ti in range(nt):
        c0 = ti * 512
        pi = pin
        P.dma('sp', pi.t[:, :, :], I['p_own'][c0:c0 + 512, :].rearrange("(s p) d -> p s d", p=128), [], [pi.b], pi.b)
        hos = {}
        hxb = {}

        def cb_pre(sub, ho):
            hos[sub] = ho
            hxb[sub] = fc.xnb[sub % 2]
            fc.norm_pre(ho.t[:, :], ho.b, hxb[sub])
            pb_ = pbf[sub % 2]
            P.op('dve', lambda e: e.tensor_copy(out=pb_.t[:, :], in_=pi.t[:, sub, :]), [pi.b], [pb_.b])

        def cb_T(sub):
            fc.norm_tr(hxb[sub], hT, sub, fc.pT[0])
            pb_ = pbf[sub % 2]
            ptp = fc.pT[0]
            for c in range(2):
                P.op('pe', lambda e, c=c: e.transpose(out=ptp.t[:, c * 128:(c + 1) * 128],
                                                      in_=pb_.t[:, c * 128:(c + 1) * 128], identity=fc.ident.t[:, :]),
                     [pb_.b, fc.ident.b], [ptp.b])
            pts = pT_sb[sub % 2]
            P.op('dve', lambda e: e.tensor_copy(out=pts.t[:, :, :],
                                                in_=ptp.t[:, 0:256].rearrange("p (c n) -> p c n", c=2)),
                 [ptp.b], [pts.b])

        def cb_mm(sub, c0=c0):
            ho = hos[sub]
            pts = pT_sb[sub % 2]
            s_ = sg
            e_ = ev[sub % 2]
            for half in range(2):
                pg = fc.nextA()
                pe_ = fc.nextA()
                for c in range(8):
                    P.op('pe', lambda e, c=c: e.matmul(pg.t[:, :], lhsT=hT.t[:, c, sub * 128:(sub + 1) * 128],
                                                       rhs=wpg.t[:, c, half * 512:(half + 1) * 512], start=(c == 0),
                                                       stop=(c == 7)), [hT.b, wpg.b], [pg.b])
                for c in range(2):
                    P.op('pe', lambda e, c=c: e.matmul(pe_.t[:, :], lhsT=pts.t[:, c, :],
                                                       rhs=wpe.t[:, c, half * 512:(half + 1) * 512], start=(c == 0),
                                                       stop=(c == 1)), [pts.b, wpe.b], [pe_.b])
                P.op('act', lambda e: e.activation(out=s_.t[:, half * 512:(half + 1) * 512], in_=pg.t[:, :],
                                                   func=AF.Sigmoid), [pg.b], [s_.b])
                P.op('dve', lambda e: e.tensor_tensor(out=e_.t[:, half * 512:(half + 1) * 512], in0=pe_.t[:, :],
                                                      in1=s_.t[:, half * 512:(half + 1) * 512], op=ALU.mult),
                     [pe_.b, s_.b], [e_.b])
            st = fc.nstat()
            P.op('act', lambda e: e.activation(out=fc.junk.t[:, :], in_=e_.t[:, :], func=AF.Square,
                                               accum_out=st.t[:, 0:1]), [e_.b], [fc.junk.b, st.b])
            rstd_from_ss(P, (st.t[:, 0:1], st.b), (st.t[:, 2:3], st.b), D, (fc.mhalf.t[:, 0:1], fc.mhalf.b),
                         (st.t[:, 1:2], st.b))
            P.op('dve', lambda e: e.scalar_tensor_tensor(out=e_.t[:, :], in0=e_.t[:, :], scalar=st.t[:, 2:3],
                                                         in1=gpe.t[:, :], op0=ALU.mult, op1=ALU.mult),
                 [e_.b, st.b, gpe.b], [e_.b])
            P.op('pool', lambda e: e.tensor_tensor(out=e_.t[:, :], in0=e_.t[:, :], in1=ho.t[:, :], op=ALU.add),
                 [e_.b, ho.b], [e_.b])
            r0 = c0 + sub * 128
            P.dma('pool', y_out[r0:r0 + 128, :], e_.t[:, :], [e_.b], [DB['y_out']], e_.b)
        nsrc = S['h2sp'][c0 + 512:c0 + 1024, :] if ti + 1 < nt else None
        fc.run_tile(ring, nsrc, cb_pre, cb_T, cb_mm)
    P.end_phase()


def rope_tables(pos):
    inv = (1.0 / (np.float32(10000.0) ** (np.arange(0, 32, 2, dtype=np.float32) / np.float32(32)))).astype(np.float32)
    ang = (pos.astype(np.float32)[:, None] * inv[None, :]).astype(np.float32)
    c = np.cos(ang.astype(np.float64)).astype(np.float32).T
    s = np.sin(ang.astype(np.float64)).astype(np.float32).T
    n = pos.shape[0]
    C = np.ones((96, n), np.float32)
    Sn = np.zeros((96, n), np.float32)
    C[64:80] = c
    C[80:96] = c
    Sn[64:80] = -s
    Sn[80:96] = s
    return C, Sn


def t5_bucket_np(rel):
    nb = 16
    max_exact = 8
    ret = np.where(rel > 0, nb, 0)
    n = np.abs(rel)
    nf = np.maximum(n, 1).astype(np.float32)
    large = max_exact + (np.log(nf / np.float32(max_exact)) / np.float32(np.log(128 / max_exact))
                         * np.float32(nb - max_exact)).astype(np.int32)
    large = np.minimum(large, nb - 1)
    return ret + np.where(n < max_exact, n, large)


def onehot_table():
    j = np.arange(128)[:, None]
    i = np.arange(128)[None, :]
    oh = np.zeros((33, 3, 128, 128), np.float32)
    for kb in range(3):
        rel = (kb - 1) * 128 + j - i
        valid = np.abs(rel) <= 128
        bk = t5_bucket_np(rel)
        for b in range(32):
            oh[b, kb] = ((bk == b) & valid).astype(np.float32)
        oh[32, kb] = (~valid).astype(np.float32)
    return oh.reshape(33, -1)


def shared_inputs(inp):
    g = lambda k: np.asarray(inp[k], np.float32)[0]
    sh = {}

    def cols(v):
        return np.ascontiguousarray(v.reshape(-1, 128).T)
    sh['gcols'] = np.concatenate([cols(g('ffn1_pre_g')), cols(g('mix_pre_g')), cols(g('q_norm_g')),
                                  cols(g('kv_norm_g')), cols(g('ffn2_pre_g')), cols(g('ple_pre_g'))], axis=1)
    gp = np.concatenate([g('ffn1_post_g'), g('mix_post_g'), g('ffn2_post_g'), g('ple_post_g')])
    sh['gpost'] = np.ascontiguousarray(np.broadcast_to(gp[None, :], (128, 4 * D)))
    sh['rb_ext'] = np.concatenate([np.asarray(inp['rel_bias'], np.float32), np.full((1, 8), NEG, np.float32)], axis=0)
    sh['onehot'] = onehot_table()
    sh['sinkb'] = np.ascontiguousarray(np.broadcast_to(g('sink')[None, :], (128, 8)))
    sh['ident'] = np.eye(128, dtype=np.float32).astype(ml_dtypes.bfloat16)
    shf = np.zeros((128, 64), np.float32)
    shf[64 + np.arange(64), np.arange(64)] = 1.0
    sh['shift'] = shf
    for n in ['ffn1', 'ffn2']:
        for w in ['w1', 'w3', 'w2']:
            sh[n + '_' + w] = g(n + '_' + w)
    w_in = g('w_in')
    o = np.cumsum([0, 384, 256, 32, 512, 128, 128, 1024, 1024])
    cq, ckv, kr, qb, kb, vb, ga, gb = [w_in[:, o[i]:o[i + 1]] for i in range(8)]
    z64 = np.zeros((D, 64), np.float32)
    z32 = np.zeros((D, 32), np.float32)
    krs = np.concatenate([kr[:, 16:32], kr[:, 0:16]], axis=1)
    sh['w_in'] = np.ascontiguousarray(np.concatenate(
        [ckv, z64, kr, z32, z64, krs, z32, kb, vb, cq, qb, ga, gb], axis=1))
    assert sh['w_in'].shape[1] == 29 * 128
    wq = g('w_uq')
    slots = []
    zq = np.zeros((384, 32), np.float32)
    for h in range(8):
        blk = wq[:, h * 96:(h + 1) * 96]
        slots.append(np.concatenate([blk, zq], axis=1))
        sw = np.concatenate([blk[:, 0:64], blk[:, 80:96], blk[:, 64:80]], axis=1)
        slots.append(np.concatenate([sw, zq], axis=1))
    sh['w_uq'] = np.ascontiguousarray(np.concatenate(slots, axis=1))
    sh['w_uk'] = g('w_uk')
    sh['w_uv'] = g('w_uv')
    sh['w_pa'] = g('w_proj_a')
    sh['w_pb'] = g('w_proj_b')
    sh['w_out'] = g('w_out')
    sh['w_pg'] = g('w_ple_gate')
    sh['w_pe'] = g('w_ple_proj')
    return sh


def core_inputs(cfg, prompts_x, prompts_p, samp_x, samp_p, r, nquart):
    SS, QS = cfg.SS, cfg.QS
    start = r * QS - 512
    idx = (start + np.arange(SS)) % SS
    xs = [np.asarray(a, np.float32) for a in prompts_x] + [np.asarray(samp_x, np.float32)[idx]]
    ps_ = [np.asarray(a, np.float32) for a in prompts_p] + [np.asarray(samp_p, np.float32)[r * QS:(r + 1) * QS]]
    Cs, Ss = [], []
    for a in prompts_x:
        C, Sn = rope_tables(np.arange(a.shape[0]))
        Cs.append(C)
        Ss.append(Sn)
    C, Sn = rope_tables(idx)
    Cs.append(C)
    Ss.append(Sn)
    edge = np.zeros((128, 2), np.float32)
    if r == 0:
        edge[:, 0] = NEG
    if r == nquart - 1:
        edge[:, 1] = NEG
    return dict(x_all=np.ascontiguousarray(np.concatenate(xs, axis=0)),
                p_own=np.ascontiguousarray(np.concatenate(ps_, axis=0)),
                ropeC=np.ascontiguousarray(np.concatenate(Cs, axis=1)),
                ropeS=np.ascontiguousarray(np.concatenate(Ss, axis=1)), edge=edge)


_NC_CACHE = {}


def kernel(**inp):
    cfg = Cfg()
    if 'nc' not in _NC_CACHE:
        _NC_CACHE['nc'] = build(cfg)
    nc = _NC_CACHE['nc']
    sh = shared_inputs(inp)
    xp = np.asarray(inp['x_prompt'], np.float32)
    xs = np.asarray(inp['x_sample'], np.float32)
    pp = np.asarray(inp['p_prompt'], np.float32)[0]
    psm = np.asarray(inp['p_sample'], np.float32)[0]
    in_maps = []
    for c in range(8):
        b, r = c // 4, c % 4
        ci = core_inputs(cfg, [xp[2 * c], xp[2 * c + 1]], [pp[2 * c], pp[2 * c + 1]], xs[b], psm[b], r, 4)
        m = dict(sh)
        m.update(ci)
        in_maps.append(m)
    res = run_bass_kernel_spmd(nc, in_maps, core_ids=list(range(8)))
    y_p = np.zeros((16, 2048, D), np.float32)
    y_s = np.zeros((2, 16384, D), np.float32)
    for c in range(8):
        y = np.asarray(res.results[c]['y_own'], np.float32)
        y_p[2 * c] = y[0:2048]
        y_p[2 * c + 1] = y[2048:4096]
        b, r = c // 4, c % 4
        y_s[b, r * 4096:(r + 1) * 4096] = y[4096:8192]
    return (y_p, y_s)
```
